# Optimizing a Trainium2 kernel written in Bass

```python
import jax
import jax.numpy as jnp
from jax import lax
import numpy as np

D_MODEL = 1024
BATCH = 16
SEQ = 4096
DEPTH = 1
DEC_BATCH = 2
DEC_SEQ = 16384
PAST_LEN = 128

HEAD_DIM = 128
ATT_HQ = 8
ATT_HKV = 2
ATT_GROUP = ATT_HQ // ATT_HKV
WINDOW = 128
ATT_BLOCK = 128
ROT_DIM = HEAD_DIM // 4
ROPE_THETA = 500000.0
DN_H = 8
DN_DK = 128
DN_DV = 128
DN_CHUNK = 64
CONV_K = 5
D_FF = ((8 * D_MODEL + 767) // 768) * 256
RMS_EPS = 1e-6

ATT_Q_W = ATT_HQ * HEAD_DIM
ATT_KV_W = ATT_HKV * HEAD_DIM
DN_QK_W = DN_H * DN_DK
DN_V_W = DN_H * DN_DV
DN_CONV_W = 2 * DN_QK_W + DN_V_W
IN_COLS = ATT_Q_W + 2 * ATT_KV_W + DN_CONV_W + DN_V_W + 4 * DN_H + 2 * D_MODEL

kernel_name = "hybrid_swa_gdn_encoder"


def _rmsnorm(x, g):
    xf = x.astype(jnp.float32)
    r = lax.rsqrt(jnp.mean(xf * xf, axis=-1, keepdims=True) + RMS_EPS)
    return (xf * r * g.astype(jnp.float32)).astype(x.dtype)


def _l2norm(x):
    return x * lax.rsqrt(jnp.sum(x * x, axis=-1, keepdims=True) + RMS_EPS)


def _partial_rope(x, pos):
    half = ROT_DIM // 2
    inv = ROPE_THETA ** (-jnp.arange(half, dtype=jnp.float32) * 2.0 / ROT_DIM)
    ang = pos.astype(jnp.float32)[:, None] * inv[None, :]
    cos = jnp.cos(ang)[None, :, None, :]
    sin = jnp.sin(ang)[None, :, None, :]
    xr = x[..., :ROT_DIM].astype(jnp.float32)
    x1, x2 = xr[..., :half], xr[..., half:]
    rot = jnp.concatenate([x1 * cos - x2 * sin, x2 * cos + x1 * sin], axis=-1)
    return jnp.concatenate([rot.astype(x.dtype), x[..., ROT_DIM:]], axis=-1)


def _window_attention(q, k, v, sink):
    B, S, _, hd = q.shape
    nb = S // ATT_BLOCK
    qb = q.astype(jnp.float32).reshape(B, nb, ATT_BLOCK, ATT_HKV, ATT_GROUP, hd)

    def band(t):
        tp = jnp.pad(t.astype(jnp.float32), ((0, 0), (ATT_BLOCK, ATT_BLOCK), (0, 0), (0, 0)))
        tp = tp.reshape(B, nb + 2, ATT_BLOCK, ATT_HKV, hd)
        return jnp.concatenate([tp[:, :-2], tp[:, 1:-1], tp[:, 2:]], axis=2)

    kw, vw = band(k), band(v)
    s = jnp.einsum("bnqhgd,bnkhd->bhgnqk", qb, kw) * (hd ** -0.5)
    qpos = jnp.arange(nb)[:, None] * ATT_BLOCK + jnp.arange(ATT_BLOCK)[None, :]
    kpos = jnp.arange(nb)[:, None] * ATT_BLOCK - ATT_BLOCK + jnp.arange(3 * ATT_BLOCK)[None, :]
    rel = kpos[:, None, :] - qpos[:, :, None]
    valid = ((kpos >= 0) & (kpos < S))[:, None, :]
    mask = (jnp.abs(rel) <= WINDOW) & valid
    s = jnp.where(mask, s, -jnp.inf)
    sk = sink.astype(jnp.float32).reshape(ATT_HKV, ATT_GROUP)[None, :, :, None, None, None]
    m = jnp.maximum(jnp.max(s, axis=-1, keepdims=True), sk)
    p = jnp.exp(s - m)
    denom = jnp.sum(p, axis=-1, keepdims=True) + jnp.exp(sk - m)
    o = jnp.einsum("bhgnqk,bnkhd->bnqhgd", p / denom, vw)
    return o.reshape(B, S, ATT_HQ * hd).astype(q.dtype)


def _delta_rule_chunked(q, k, v, g, beta):
    B, S, H, Dk = q.shape
    Dv = v.shape[-1]
    C = DN_CHUNK
    N = S // C

    def blk(t):
        return jnp.moveaxis(t.reshape((B, N, C, H) + t.shape[3:]), 3, 1)

    q, k, v, g, beta = blk(q), blk(k), blk(v), blk(g), blk(beta)
    G = jnp.cumsum(g, axis=-1)
    idx = jnp.arange(C)
    incl = idx[:, None] >= idx[None, :]
    strict = idx[:, None] > idx[None, :]
    decay = jnp.exp(jnp.where(incl, G[..., :, None] - G[..., None, :], -jnp.inf))
    kk = jnp.einsum("bhnid,bhnjd->bhnij", k, k)
    A = jnp.where(strict, beta[..., :, None] * kk * decay, 0.0)
    IA = A + jnp.eye(C, dtype=A.dtype)
    eG = jnp.exp(G)
    W = lax.linalg.triangular_solve(IA, (beta * eG)[..., None] * k, left_side=True, lower=True, unit_diagonal=True)
    U = lax.linalg.triangular_solve(IA, beta[..., None] * v, left_side=True, lower=True, unit_diagonal=True)
    QK = jnp.einsum("bhnid,bhnjd->bhnij", q, k) * decay
    q_dec = q * eG[..., None]
    k_dec = k * jnp.exp(G[..., -1:] - G)[..., None]
    g_last = jnp.exp(G[..., -1])
    xs = (jnp.moveaxis(W, 2, 0), jnp.moveaxis(U, 2, 0), jnp.moveaxis(QK, 2, 0),
          jnp.moveaxis(q_dec, 2, 0), jnp.moveaxis(k_dec, 2, 0), jnp.moveaxis(g_last, 2, 0))

    def step(state, inp):
        w_n, u_n, qk_n, qd_n, kd_n, gl_n = inp
        v_new = u_n - jnp.einsum("bhcd,bhde->bhce", w_n, state)
        o = jnp.einsum("bhcd,bhde->bhce", qd_n, state) + jnp.einsum("bhij,bhje->bhie", qk_n, v_new)
        state = state * gl_n[..., None, None] + jnp.einsum("bhcd,bhce->bhde", kd_n, v_new)
        return state, o

    s0 = jnp.zeros((B, H, Dk, Dv), jnp.float32)
    _, o = lax.scan(step, s0, xs)
    return jnp.transpose(o, (1, 0, 3, 2, 4)).reshape(B, S, H, Dv)


def _gated_deltanet(qkv, z, a, b, conv_w, a_log, dt_bias, norm_w):
    B, S, _ = qkv.shape
    c = lax.conv_general_dilated(qkv.astype(jnp.float32), conv_w.astype(jnp.float32)[:, None, :],
                                 window_strides=(1,), padding=[(CONV_K // 2, CONV_K // 2)],
                                 dimension_numbers=("NWC", "WIO", "NWC"),
                                 feature_group_count=DN_CONV_W)
    c = jax.nn.silu(c)
    q = _l2norm(c[..., :DN_QK_W].reshape(B, S, DN_H, DN_DK)) * (DN_DK ** -0.5)
    k = _l2norm(c[..., DN_QK_W:2 * DN_QK_W].reshape(B, S, DN_H, DN_DK))
    v = c[..., 2 * DN_QK_W:].reshape(B, S, DN_H, DN_DV)
    a = a.astype(jnp.float32).reshape(B, S, 2, DN_H)
    b = b.astype(jnp.float32).reshape(B, S, 2, DN_H)
    g = -jnp.exp(a_log.astype(jnp.float32)) * jax.nn.softplus(a + dt_bias.astype(jnp.float32))
    beta = jax.nn.sigmoid(b)
    o_f = _delta_rule_chunked(q, k, v, g[:, :, 0], beta[:, :, 0])
    fl = lambda t: jnp.flip(t, axis=1)
    o_b = fl(_delta_rule_chunked(fl(q), fl(k), fl(v), fl(g[:, :, 1]), fl(beta[:, :, 1])))
    o = o_f + o_b
    o = o * lax.rsqrt(jnp.mean(o * o, axis=-1, keepdims=True) + RMS_EPS) * norm_w.astype(jnp.float32)
    o = o * jax.nn.silu(z.astype(jnp.float32).reshape(B, S, DN_H, DN_DV))
    return o.reshape(B, S, DN_V_W).astype(qkv.dtype)


def _layer(x, w_in, conv_w, a_log, dt_bias, attn_sink, dn_norm_w, w_attn_o, w_dn_o, w_out,
           w_gate, w_up, w_down, g_pre_mix, g_post_mix, g_pre_ffn, g_post_ffn):
    B, S, _ = x.shape
    h = _rmsnorm(x, g_pre_mix)
    p = h @ w_in
    o0 = ATT_Q_W
    o1 = o0 + ATT_KV_W
    o2 = o1 + ATT_KV_W
    o3 = o2 + DN_CONV_W
    o4 = o3 + DN_V_W
    o5 = o4 + 2 * DN_H
    o6 = o5 + 2 * DN_H
    pos = jnp.arange(S)
    q_a = _partial_rope(p[..., :o0].reshape(B, S, ATT_HQ, HEAD_DIM), pos)
    k_a = _partial_rope(p[..., o0:o1].reshape(B, S, ATT_HKV, HEAD_DIM), pos)
    v_a = p[..., o1:o2].reshape(B, S, ATT_HKV, HEAD_DIM)
    attn = _window_attention(q_a, k_a, v_a, attn_sink)
    dn = _gated_deltanet(p[..., o2:o3], p[..., o3:o4], p[..., o4:o5], p[..., o5:o6],
                         conv_w, a_log, dt_bias, dn_norm_w)
    gate_attn = jax.nn.sigmoid(p[..., o6:o6 + D_MODEL])
    gate_dn = jax.nn.sigmoid(p[..., o6 + D_MODEL:])
    mixed = gate_attn * (attn @ w_attn_o) + gate_dn * (dn @ w_dn_o)
    x = x + _rmsnorm(mixed @ w_out, g_post_mix)
    h = _rmsnorm(x, g_pre_ffn)
    f = (jax.nn.silu(h @ w_gate) * (h @ w_up)) @ w_down
    return x + _rmsnorm(f, g_post_ffn)


def setup_inputs(seed: int = 0) -> dict:
    key = jax.random.key(seed)
    ks = jax.random.split(key, 24)
    f32 = jnp.float32

    def nrm(k, shape, fan_in):
        return jax.random.normal(k, shape, f32) * (fan_in ** -0.5)

    def gain(k):
        return 1.0 + 0.02 * jax.random.normal(k, (DEPTH, D_MODEL), f32)

    dt = jnp.exp(jax.random.uniform(ks[5], (DEPTH, 2, DN_H), f32, minval=np.log(1e-3), maxval=np.log(1e-1)))
    return {
        "x_prompt": jax.random.normal(ks[0], (BATCH, SEQ, D_MODEL), f32),
        "x_sample": jax.random.normal(ks[1], (DEC_BATCH, DEC_SEQ, D_MODEL), f32),
        "w_in": nrm(ks[2], (DEPTH, D_MODEL, IN_COLS), D_MODEL),
        "conv_w": nrm(ks[3], (DEPTH, CONV_K, DN_CONV_W), CONV_K),
        "a_log": jnp.log(jax.random.uniform(ks[4], (DEPTH, 2, DN_H), f32, minval=1.0, maxval=16.0)),
        "dt_bias": dt + jnp.log(-jnp.expm1(-dt)),
        "attn_sink": 0.5 * jax.random.normal(ks[6], (DEPTH, ATT_HQ), f32),
        "dn_norm_w": 1.0 + 0.02 * jax.random.normal(ks[7], (DEPTH, DN_DV), f32),
        "w_attn_o": nrm(ks[8], (DEPTH, ATT_Q_W, D_MODEL), ATT_Q_W),
        "w_dn_o": nrm(ks[9], (DEPTH, DN_V_W, D_MODEL), DN_V_W),
        "w_out": nrm(ks[10], (DEPTH, D_MODEL, D_MODEL), D_MODEL),
        "w_gate": nrm(ks[11], (DEPTH, D_MODEL, D_FF), D_MODEL),
        "w_up": nrm(ks[12], (DEPTH, D_MODEL, D_FF), D_MODEL),
        "w_down": nrm(ks[13], (DEPTH, D_FF, D_MODEL), D_FF),
        "g_pre_mix": gain(ks[14]),
        "g_post_mix": gain(ks[15]),
        "g_pre_ffn": gain(ks[16]),
        "g_post_ffn": gain(ks[17]),
    }


def reference(x_prompt, x_sample, w_in, conv_w, a_log, dt_bias, attn_sink, dn_norm_w, w_attn_o,
              w_dn_o, w_out, w_gate, w_up, w_down, g_pre_mix, g_post_mix, g_pre_ffn, g_post_ffn):
    y_prompt = x_prompt
    y_sample = x_sample
    for l in range(DEPTH):
        y_prompt = _layer(y_prompt, w_in[l], conv_w[l], a_log[l], dt_bias[l], attn_sink[l], dn_norm_w[l],
                          w_attn_o[l], w_dn_o[l], w_out[l], w_gate[l], w_up[l], w_down[l],
                          g_pre_mix[l], g_post_mix[l], g_pre_ffn[l], g_post_ffn[l])
        y_sample = _layer(y_sample, w_in[l], conv_w[l], a_log[l], dt_bias[l], attn_sink[l], dn_norm_w[l],
                          w_attn_o[l], w_dn_o[l], w_out[l], w_gate[l], w_up[l], w_down[l],
                          g_pre_mix[l], g_post_mix[l], g_pre_ffn[l], g_post_ffn[l])
    return (y_prompt, y_sample)
```

```python
import contextlib
import os
import numpy as np
import concourse.bass as bass
import concourse.mybir as mybir
from concourse.bass_utils import run_bass_kernel_spmd

F32 = mybir.dt.float32
BF16 = mybir.dt.bfloat16
F32R = mybir.dt.float32r
ALU = mybir.AluOpType
AF = mybir.ActivationFunctionType
AX = mybir.AxisListType

PE, ACT, DVE, POOL, SP = "tensor", "scalar", "vector", "gpsimd", "sync"
ENGS = (PE, ACT, DVE, POOL, SP)

D_MODEL = 1024
HD = 128
NHQ = 8
NHKV = 2
DNH = 8
D_FF = 2816
NFF = D_FF // 128
IN_COLS = 7712
EPS = 1e-6
NEG = -30000.0
C_QA, C_KA, C_VA, C_DN, C_Z, C_AB, C_GA = 0, 1024, 1280, 1536, 4608, 5632, 5664


class Tok:
    __slots__ = ("name", "w", "rs")

    def __init__(self, name=""):
        self.name = name
        self.w = None
        self.rs = []


class Op:
    __slots__ = ("eng", "fn", "waits", "sig", "dma_key", "val", "is_dma", "ndma", "seq")


class Prog:
    def __init__(self, nc):
        self.nc = nc
        self.by_eng = {e: [] for e in ENGS}
        self.dma_cnt = {}
        self.dma_last = {}
        self.nops = 0

    def add(self, eng, fn, reads=(), writes=(), dma_key=None, ndma=1):
        op = Op()
        op.eng = eng
        op.fn = fn
        op.sig = False
        op.is_dma = dma_key is not None
        op.dma_key = dma_key
        op.ndma = ndma
        op.val = None
        op.seq = self.nops
        deps = {}
        for t in reads:
            if t.w is not None:
                deps[id(t.w)] = (t.w, True)
        for t in writes:
            if t.w is not None and id(t.w) not in deps:
                deps[id(t.w)] = (t.w, False)
            for r in t.rs:
                if id(r) not in deps:
                    deps[id(r)] = (r, False)
        if op.is_dma:
            prev = self.dma_last.get(dma_key)
            if prev is not None:
                deps[id(prev)] = (prev, True)
        best = {}
        for p, raw in deps.values():
            if p.is_dma:
                key = ("d", p.dma_key)
            elif op.is_dma or p.eng != op.eng or raw or op.eng != PE:
                key = ("e", p.eng)
            else:
                continue
            q = best.get(key)
            if q is None or p.seq > q.seq:
                best[key] = p
        waits = []
        for p in best.values():
            if not p.is_dma:
                p.sig = True
            waits.append(p)
        op.waits = waits
        for t in reads:
            t.rs.append(op)
        for t in writes:
            t.w = op
            t.rs = []
        if op.is_dma:
            self.dma_cnt[dma_key] = self.dma_cnt.get(dma_key, 0) + 16 * ndma
            op.val = self.dma_cnt[dma_key]
            self.dma_last[dma_key] = op
        self.by_eng[eng].append(op)
        self.nops += 1
        return op

    def barrier(self):
        lasts = []
        for e in ENGS:
            for op in reversed(self.by_eng[e]):
                if not op.is_dma and op.fn is not None:
                    op.sig = True
                    lasts.append(op)
                    break
        lasts += list(self.dma_last.values())
        for e in ENGS:
            op = Op()
            op.eng = e
            op.fn = None
            op.sig = False
            op.is_dma = False
            op.dma_key = None
            op.ndma = 0
            op.val = None
            op.seq = self.nops
            self.nops += 1
            op.waits = list(lasts)
            self.by_eng[e].append(op)

    def emit(self, final_waits=()):
        nc = self.nc
        for e in ENGS:
            c = 0
            for op in self.by_eng[e]:
                if not op.is_dma and op.sig:
                    c += 1
                    op.val = c
        keys = sorted(self.dma_cnt.keys())
        with contextlib.ExitStack() as st:
            esem = {e: st.enter_context(nc.semaphore("s_" + e)) for e in ENGS}
            dsem = {k: st.enter_context(nc.semaphore("d_%d" % i)) for i, k in enumerate(keys)}
            block = st.enter_context(nc.Block())

            def semof(p):
                return dsem[p.dma_key] if p.is_dma else esem[p.eng]

            def run(e, engobj, extra=None):
                waited = {}
                for op in self.by_eng[e]:
                    for p in op.waits:
                        s = semof(p)
                        k = id(s)
                        if waited.get(k, 0) >= p.val:
                            continue
                        waited[k] = p.val
                        engobj.wait_ge(s, p.val)
                    if op.fn is None:
                        continue
                    r = op.fn(engobj)
                    if op.is_dma:
                        rs = r if isinstance(r, (list, tuple)) else [r]
                        assert len(rs) == op.ndma, (len(rs), op.ndma)
                        for ins in rs:
                            ins.then_inc(dsem[op.dma_key], 16)
                    elif op.sig:
                        ins = r[-1] if isinstance(r, (list, tuple)) else r
                        ins.then_inc(esem[e], 1)
                if extra:
                    for p in extra:
                        s = semof(p)
                        if waited.get(id(s), 0) >= p.val:
                            continue
                        waited[id(s)] = p.val
                        engobj.wait_ge(s, p.val)

            @block.tensor
            def _(eng):
                run(PE, eng)

            @block.scalar
            def _(eng):
                run(ACT, eng)

            @block.vector
            def _(eng):
                run(DVE, eng)

            @block.gpsimd
            def _(eng):
                run(POOL, eng)

            @block.sync
            def _(eng):
                run(SP, eng, extra=list(final_waits))


class Ring:
    def __init__(self, bufs):
        self.bufs = bufs
        self.i = 0

    def next(self):
        b = self.bufs[self.i % len(self.bufs)]
        self.i += 1
        return b


def in_blocks():
    bl = []
    for h in range(NHQ):
        bl.append(("qa%d" % h, C_QA + h * 128, 128))
    for g in range(NHKV):
        bl.append(("ka%d" % g, C_KA + g * 128, 128))
    bl.append(("va", C_VA, 256))
    for c in range(24):
        bl.append(("dn%d" % c, C_DN + c * 128, 128))
    bl.append(("z0", C_Z, 512))
    bl.append(("z1", C_Z + 512, 512))
    bl.append(("ab", C_AB, 32))
    for c in range(16):
        bl.append(("g%d" % c, C_GA + c * 128, 128))
    return bl


class Ctx:
    pass


def build(T, SEG, BLK1, debug=False, phases=(0, 1, 2, 3, 4)):
    assert T % SEG == 0 and SEG % BLK1 == 0 and BLK1 % 512 == 0
    NT = T // 128
    NB1 = T // BLK1
    nc = bass.Bass("TRN2", target_bir_lowering=False)
    K = Ctx()
    K.nc, K.T, K.SEG, K.BLK1, K.NT, K.NB1 = nc, T, SEG, BLK1, NT, NB1
    K.debug = debug

    def din(name, shape, dt=F32):
        return nc.dram_tensor(name, list(shape), dt, kind="ExternalInput").ap()

    def dscr(name, shape, dt):
        kind = "ExternalOutput" if debug else "Internal"
        return nc.dram_tensor(name, list(shape), dt, kind=kind).ap()

    I = {}
    I["x"] = din("x", [T, D_MODEL])
    I["w_in"] = din("w_in", [D_MODEL, IN_COLS])
    I["conv_w"] = din("conv_w", [5, 3072])
    I["a_log"] = din("a_log", [16])
    I["dt_bias"] = din("dt_bias", [16])
    I["attn_sink"] = din("attn_sink", [8])
    I["dn_norm_w"] = din("dn_norm_w", [128])
    I["w_attn_o"] = din("w_attn_o", [1024, 1024])
    I["w_dn_o"] = din("w_dn_o", [1024, 1024])
    I["w_out"] = din("w_out", [1024, 1024])
    I["w_gate"] = din("w_gate", [1024, D_FF])
    I["w_up"] = din("w_up", [1024, D_FF])
    I["w_down"] = din("w_down", [D_FF, 1024])
    for g in ("g_pre_mix", "g_post_mix", "g_pre_ffn", "g_post_ffn"):
        I[g] = din(g, [1024])
    I["ident"] = din("ident", [128, 128])
    I["rotm"] = din("rotm", [128, 128])
    I["cosT"] = din("cosT", [32, T])
    I["sinT"] = din("sinT", [32, T])
    I["hflags"] = din("hflags", [4, NB1])
    I["carry"] = din("carry", [128, 1])
    I["amask"] = din("amask", [128, 5, 384])
    I["tri"] = din("tri", [128, 6, 128])
    K.I = I
    y = nc.dram_tensor("y", [T, D_MODEL], F32, kind="ExternalOutput").ap()
    K.y = y

    S = {}
    S["Wb_in"] = nc.dram_tensor("Wb_in", [128, 8 * IN_COLS], BF16, kind="Internal").ap()
    for nm in ("Wb_ao", "Wb_do", "Wb_out"):
        S[nm] = nc.dram_tensor(nm, [128, 8 * 1024], BF16, kind="Internal").ap()
    for nm in ("Wb_g", "Wb_u"):
        S[nm] = nc.dram_tensor(nm, [128, 8 * D_FF], BF16, kind="Internal").ap()
    S["Wb_d"] = nc.dram_tensor("Wb_d", [128, NFF * 1024], BF16, kind="Internal").ap()
    S["QA"] = dscr("QA", [8, 128, T], BF16)
    S["KA"] = dscr("KA", [2, 128, T], BF16)
    S["VA"] = dscr("VA", [T, 256], BF16)
    S["DQ"] = dscr("DQ", [8, 128, T], BF16)
    S["DK"] = dscr("DK", [8, 128, T], BF16)
    S["DKt"] = dscr("DKt", [T, 1024], BF16)
    S["DV"] = dscr("DV", [T, 1024], BF16)
    S["ZS"] = dscr("ZS", [T, 1024], F32)
    S["AB"] = dscr("AB", [T, 32], F32)
    S["GT"] = dscr("GT", [16, 128, T], F32)
    S["AT"] = dscr("AT", [8, 128, T], BF16)
    S["OF"] = dscr("OF", [T, 1024], F32)
    S["OB"] = dscr("OB", [T, 1024], F32)
    K.S = S
    K.tS = {k: Tok(k) for k in S}

    P = Prog(nc)
    K.P = P
    finals = []
    with contextlib.ExitStack() as gst:
        K.sb = lambda n, s, d: gst.enter_context(nc.sbuf_tensor(n, list(s), d))
        C = Ctx()
        K.C = C
        C.ident = K.sb("c_ident", [128, 128], F32)
        C.identb = K.sb("c_identb", [128, 128], BF16)
        C.onesb = K.sb("c_onesb", [128, 128], BF16)
        C.onesf = K.sb("c_onesf", [128, 128], F32)
        C.eps = K.sb("c_eps", [128, 1], F32)
        C.carry = K.sb("c_carry", [128, 1], F32)
        C.t_const = Tok("const")
        P.add(SP, lambda e: e.dma_start(out=C.ident[:], in_=I["ident"]), writes=[C.t_const], dma_key="c0")
        P.add(SP, lambda e: e.dma_start(out=C.carry[:], in_=I["carry"]), writes=[C.t_const], dma_key="c1")
        P.add(DVE, lambda e: e.tensor_copy(out=C.identb[:], in_=C.ident[:]), reads=[C.t_const], writes=[C.t_const])
        P.add(DVE, lambda e: e.memset(C.onesb[:], 1.0), writes=[C.t_const])
        P.add(DVE, lambda e: e.memset(C.onesf[:], 1.0), writes=[C.t_const])
        P.add(DVE, lambda e: e.memset(C.eps[:], EPS), writes=[C.t_const])
        if 0 in phases:
            phase0(K)
        if 1 in phases:
            phase1(K)
            P.barrier()
        if 2 in phases:
            phase2(K)
            P.barrier()
        if 3 in phases:
            phase3(K)
            P.barrier()
        if 4 in phases:
            finals += phase4(K)
        finals += getattr(K, "finals", [])
        P.emit(final_waits=finals)
    return nc


def phase0(K):
    P, I, S = K.P, K.I, K.S
    K.t_w = {}
    off = 0
    K.in_off = {}
    n = 0
    for (nm, c0, ncol) in in_blocks():
        K.in_off[nm] = (off, ncol)
        dst = S["Wb_in"][:, off:off + 8 * ncol].rearrange("p (kc c) -> p kc c", kc=8)
        src = I["w_in"][:, c0:c0 + ncol].rearrange("(kc p) c -> p kc c", p=128)
        t = Tok("w_" + nm)
        K.t_w["in_" + nm] = t
        P.add(POOL, lambda e, dst=dst, src=src: e.dma_start(out=dst, in_=src), writes=[t], dma_key="wc%d" % (n % 4))
        n += 1
        off += 8 * ncol
    for (sn, wn) in (("Wb_ao", "w_attn_o"), ("Wb_do", "w_dn_o"), ("Wb_out", "w_out")):
        for c in range(8):
            dst = S[sn][:, c * 1024:(c + 1) * 1024].rearrange("p (kc c) -> p kc c", kc=8)
            src = I[wn][:, c * 128:(c + 1) * 128].rearrange("(kc p) c -> p kc c", p=128)
            t = Tok()
            K.t_w["%s_%d" % (sn, c)] = t
            P.add(POOL, lambda e, dst=dst, src=src: e.dma_start(out=dst, in_=src), writes=[t], dma_key="wc%d" % (n % 4))
            n += 1
    for (sn, wn) in (("Wb_g", "w_gate"), ("Wb_u", "w_up")):
        for f in range(NFF):
            dst = S[sn][:, f * 1024:(f + 1) * 1024].rearrange("p (kc c) -> p kc c", kc=8)
            src = I[wn][:, f * 128:(f + 1) * 128].rearrange("(kc p) c -> p kc c", p=128)
            t = Tok()
            K.t_w["%s_%d" % (sn, f)] = t
            P.add(POOL, lambda e, dst=dst, src=src: e.dma_start(out=dst, in_=src), writes=[t], dma_key="wc%d" % (n % 4))
            n += 1
    for c in range(8):
        dst = S["Wb_d"][:, c * NFF * 128:(c + 1) * NFF * 128].rearrange("p (f c) -> p f c", f=NFF)
        src = I["w_down"][:, c * 128:(c + 1) * 128].rearrange("(f p) c -> p f c", p=128)
        t = Tok()
        K.t_w["Wb_d_%d" % c] = t
        P.add(POOL, lambda e, dst=dst, src=src: e.dma_start(out=dst, in_=src), writes=[t], dma_key="wc%d" % (n % 4))
        n += 1
    K.finals = [t.w for t in phase_barrier(K, ["wc%d" % i for i in range(4)])]


def phase1(K):
    nc, P, I, S, C = K.nc, K.P, K.I, K.S, K.C
    T, BLK1, NB1 = K.T, K.BLK1, K.NB1
    NTB = BLK1 // 128
    NMT = BLK1 // 512
    HW = BLK1 + 4
    with contextlib.ExitStack() as st:
        sb = lambda n, s, d: st.enter_context(nc.sbuf_tensor("p1_" + n, list(s), d))
        ps = lambda n, s, d: st.enter_context(nc.psum_tensor("p1_" + n, list(s), d))

        def mk(n, s, d, k=1, f=sb):
            return Ring([(f("%s%d" % (n, i), s, d), Tok("%s%d" % (n, i))) for i in range(k)])

        gexp = sb("gexp", [128, 8, 128], F32)
        gT = sb("gT", [128, 8], F32)
        cw = sb("cw", [128, 120], F32)
        rotm = sb("rotm", [128, 128], F32)
        hfl = sb("hfl", [4, NB1], F32)
        alog = sb("alog", [128, 16], F32)
        dtb = sb("dtb", [128, 16], F32)
        nA = sb("nA", [128, 16], F32)
        t_c1 = Tok("p1const")
        ldc = sb("ldc", [128, 128], F32)
        t_ldc = Tok("ldc")
        with contextlib.ExitStack() as st0:
            pc = st0.enter_context(nc.psum_tensor("p1_pc", [128, 128], F32))
            t_pc = Tok("pc")
            P.add(SP, lambda e: e.dma_start(out=ldc[0:8, :], in_=I["g_pre_mix"].rearrange("(kc p) -> kc p", p=128)), writes=[t_ldc], dma_key="c0")
            P.add(PE, lambda e: e.transpose(out=pc[:, 0:8], in_=ldc[0:8, :], identity=C.ident[0:8, 0:8]), reads=[t_ldc, C.t_const], writes=[t_pc])
            P.add(DVE, lambda e: e.tensor_copy(out=gT[:], in_=pc[:, 0:8]), reads=[t_pc], writes=[t_c1])
            P.add(SP, lambda e: e.dma_start(out=ldc[0:120, :], in_=I["conv_w"].rearrange("j (c p) -> (j c) p", p=128)), reads=[t_pc], writes=[t_ldc], dma_key="c0")
            P.add(PE, lambda e: e.transpose(out=pc[:, 0:120], in_=ldc[0:120, :], identity=C.ident[0:120, 0:120]), reads=[t_ldc, C.t_const], writes=[t_pc])
            P.add(DVE, lambda e: e.tensor_copy(out=cw[:], in_=pc[:, 0:120]), reads=[t_pc], writes=[t_c1])
        P.add(SP, lambda e: e.dma_start(out=rotm[:], in_=I["rotm"]), writes=[t_c1], dma_key="c0")
        P.add(SP, lambda e: e.dma_start(out=hfl[:], in_=I["hflags"]), writes=[t_c1], dma_key="c1")
        P.add(SP, lambda e: e.dma_start(out=alog[:], in_=I["a_log"].partition_broadcast(128)), writes=[t_c1], dma_key="c0")
        P.add(SP, lambda e: e.dma_start(out=dtb[:], in_=I["dt_bias"].partition_broadcast(128)), writes=[t_c1], dma_key="c1")
        P.add(DVE, lambda e: e.tensor_copy(out=gexp[:], in_=gT[:].unsqueeze(2).broadcast_to([128, 8, 128])), reads=[t_c1], writes=[t_c1])
        P.add(ACT, lambda e: e.activation(out=nA[:], in_=alog[:], func=AF.Exp), reads=[t_c1], writes=[t_c1])
        P.add(DVE, lambda e: e.tensor_scalar(out=nA[:], in0=nA[:], scalar1=-1.0, scalar2=None, op0=ALU.mult), reads=[t_c1], writes=[t_c1])

        xin = mk("xin", [128, 1024], F32, 3)
        xh = mk("xh", [4, 1024], F32, 1)
        junk = mk("junk", [128, 1024], BF16, 2)
        st1 = mk("st1", [128, 4], F32, 3)
        xn = mk("xn", [128, 1024], F32, 2)
        hT = sb("hT", [128, 8, HW], BF16)
        t_hT = Tok("hT")
        wblk = mk("wblk", [128, 8, 512], BF16, 3)
        pmain = mk("pm", [128, 512], F32, 4, ps)
        ptr = mk("ptr", [128, 4, 128], F32, 1, ps)
        paux = mk("paux", [128, 512], F32, 2, ps)
        ptb = mk("ptb", [128, 8, 128], BF16, 1, ps)
        qst = mk("qst", [128, BLK1], BF16, 2)
        pa32 = mk("pa32", [128, 512], F32, 12)
        rt1 = mk("rt1", [32, 512], F32, 2)
        rt2 = mk("rt2", [32, 512], F32, 2)
        cst = mk("cst", [32, BLK1], F32, 1)
        snt = mk("snt", [32, BLK1], F32, 1)
        pqc = mk("pqc", [128, HW], BF16, 7)
        acc = mk("acc", [128, BLK1], F32, 2)
        sqb = mk("sqb", [128, BLK1], BF16, 1)
        sd = mk("sd", [128, 512], F32, 2)
        nst = mk("nst", [128, BLK1], BF16, 2)
        vst = mk("vst", [128, NTB, 128], BF16, 1)
        kst = mk("kst", [128, NTB, 128], BF16, 1)
        gst = mk("gst", [128, 512], F32, 3)
        zst = mk("zst", [128, 512], F32, 3)
        vast = mk("vast", [128, 256], BF16, 2)
        abt = mk("abt", [128, 32], F32, 2)
        abo = mk("abo", [128, 32], F32, 2)

        blocks = in_blocks()
        nstore = [0]

        def store(out, in_, rd):
            k = "st%d" % (nstore[0] % 4)
            nstore[0] += 1
            return P.add(POOL, lambda e: e.dma_start(out=out, in_=in_), reads=rd, dma_key=k)

        for b in range(NB1):
            tb0 = b * BLK1
            def norm_rows(xt, t_x, npart, flag_ap):
                (jk, t_jk) = junk.next()
                (s1, t_s1) = st1.next()
                P.add(ACT, lambda e: e.activation(out=jk[0:npart, :], in_=xt[0:npart, :], func=AF.Square, accum_out=s1[0:npart, 0:1]),
                      reads=[t_x], writes=[t_jk, t_s1])
                P.add(ACT, lambda e: e.activation(out=s1[0:npart, 1:2], in_=s1[0:npart, 0:1], func=AF.Sqrt, bias=C.eps[0:npart, :], scale=1.0 / D_MODEL),
                      reads=[t_s1, C.t_const], writes=[t_s1])
                P.add(DVE, lambda e: e.reciprocal(out=s1[0:npart, 2:3], in_=s1[0:npart, 1:2]), reads=[t_s1], writes=[t_s1])
                if flag_ap is not None:
                    P.add(DVE, lambda e: e.tensor_tensor(out=s1[0:npart, 2:3], in0=s1[0:npart, 2:3], in1=flag_ap, op=ALU.mult),
                          reads=[t_s1, t_c1], writes=[t_s1])
                (xo, t_xo) = xn.next()
                P.add(DVE, lambda e: e.tensor_scalar(out=xo[0:npart, :], in0=xt[0:npart, :], scalar1=s1[0:npart, 2:3], scalar2=None, op0=ALU.mult),
                      reads=[t_x, t_s1], writes=[t_xo])
                return xo, t_xo

            for i in range(NTB):
                (xt, t_x) = xin.next()
                r0 = tb0 + i * 128
                P.add(SP, lambda e, xt=xt, r0=r0: e.dma_start(out=xt[:], in_=I["x"][r0:r0 + 128, :]), writes=[t_x], dma_key="xin%d" % ((xin.i - 1) % 3))
                xo, t_xo = norm_rows(xt, t_x, 128, None)
                for half in range(2):
                    (pt, t_pt) = ptr.next()
                    for j in range(4):
                        kc = half * 4 + j
                        P.add(PE, lambda e, pt=pt, xo=xo, kc=kc, j=j: e.transpose(out=pt[:, j, :], in_=xo[:, kc * 128:(kc + 1) * 128], identity=C.ident[:]),
                              reads=[t_xo, C.t_const], writes=[t_pt])
                    P.add(DVE, lambda e, pt=pt, half=half, i=i: e.tensor_tensor(out=hT[:, half * 4:(half + 1) * 4, i * 128:(i + 1) * 128], in0=pt[:],
                                                                               in1=gexp[:, half * 4:(half + 1) * 4, :], op=ALU.mult),
                          reads=[t_pt, t_c1], writes=[t_hT])
            (xt, t_x) = xh.next()
            lo = max(tb0 - 2, 0)
            hi = min(tb0 + BLK1, T - 2)
            P.add(SP, lambda e, xt=xt, lo=lo, hi=hi: [e.dma_start(out=xt[0:2, :], in_=I["x"][lo:lo + 2, :]), e.dma_start(out=xt[2:4, :], in_=I["x"][hi:hi + 2, :])],
                  writes=[t_x], dma_key="xh", ndma=2)
            xo, t_xo = norm_rows(xt, t_x, 4, hfl[:, b:b + 1])
            for half in range(2):
                (pt, t_pt) = ptr.next()
                for j in range(4):
                    kc = half * 4 + j
                    P.add(PE, lambda e, pt=pt, xo=xo, kc=kc, j=j: e.transpose(out=pt[:, j, 0:4], in_=xo[0:4, kc * 128:(kc + 1) * 128], identity=C.ident[0:4, 0:4]),
                          reads=[t_xo, C.t_const], writes=[t_pt])
                for side in range(2):
                    c0 = BLK1 + 2 * side
                    P.add(DVE, lambda e, pt=pt, half=half, side=side, c0=c0: e.tensor_tensor(out=hT[:, half * 4:(half + 1) * 4, c0:c0 + 2], in0=pt[:, :, side * 2:side * 2 + 2],
                                                                                          in1=gexp[:, half * 4:(half + 1) * 4, 0:2], op=ALU.mult),
                          reads=[t_pt, t_c1], writes=[t_hT])
            (cs, t_cs) = cst.next()
            (sn, t_sn) = snt.next()
            P.add(SP, lambda e, cs=cs, tb0=tb0: e.dma_start(out=cs[:], in_=I["cosT"][:, tb0:tb0 + BLK1]), writes=[t_cs], dma_key="cst")
            P.add(SP, lambda e, sn=sn, tb0=tb0: e.dma_start(out=sn[:], in_=I["sinT"][:, tb0:tb0 + BLK1]), writes=[t_sn], dma_key="snt")

            def wblock(nm, c0, ncol, tb0=tb0, cs=cs, t_cs=t_cs, sn=sn, t_sn=t_sn):
                (wb, t_wb) = wblk.next()
                off, _ = K.in_off[nm]
                src = S["Wb_in"][:, off:off + 8 * ncol].rearrange("p (kc c) -> p kc c", kc=8)
                P.add(SP, lambda e, wb=wb, src=src, ncol=ncol: e.dma_start(out=wb[:, :, 0:ncol], in_=src), reads=[K.t_w["in_" + nm]], writes=[t_wb],
                      dma_key="wblk%d" % ((wblk.i - 1) % 3))
                kind = nm[:2]
                if kind in ("qa", "ka", "dn") or nm[0] == "g":
                    p32s = []
                    if kind == "dn":
                        (pq, t_pq) = pqc.next()
                    for mt in range(NMT):
                        (pm, t_pm) = pmain.next()
                        for kc in range(8):
                            P.add(PE, lambda e, pm=pm, wb=wb, kc=kc, mt=mt: e.matmul(out=pm[:], lhsT=wb[:, kc, 0:128], rhs=hT[:, kc, mt * 512:(mt + 1) * 512],
                                                                                start=(kc == 0), stop=(kc == 7)),
                                  reads=[t_wb, t_hT], writes=[t_pm])
                        csl = slice(mt * 512, (mt + 1) * 512)
                        if kind in ("qa", "ka"):
                            (p32, t_p32) = pa32.next()
                            P.add(ACT, lambda e, pm=pm, p32=p32: e.activation(out=p32[:], in_=pm[:], func=AF.Copy), reads=[t_pm], writes=[t_p32])
                            p32s.append((p32, t_p32, csl))
                        elif kind == "dn":
                            P.add(ACT, lambda e, pm=pm, pq=pq, mt=mt: e.activation(out=pq[:, 2 + mt * 512:2 + (mt + 1) * 512], in_=pm[:], func=AF.Copy),
                                  reads=[t_pm], writes=[t_pq])
                        else:
                            cidx = int(nm[1:])
                            (gs, t_gs) = gst.next()
                            P.add(ACT, lambda e, pm=pm, gs=gs: e.activation(out=gs[:], in_=pm[:], func=AF.Sigmoid), reads=[t_pm], writes=[t_gs])
                            store(S["GT"][cidx, :, tb0 + mt * 512:tb0 + (mt + 1) * 512], gs[:], [t_gs])
                    if kind in ("qa", "ka"):
                        yield
                        (qs, t_qs) = qst.next()
                        for (p32, t_p32, csl) in p32s:
                            P.add(DVE, lambda e, p32=p32, qs=qs, csl=csl: e.tensor_copy(out=qs[:, csl], in_=p32[:]), reads=[t_p32], writes=[t_qs])
                            (px, t_px) = paux.next()
                            P.add(PE, lambda e, px=px, p32=p32: e.matmul(out=px[:], lhsT=rotm[:], rhs=p32[:], start=True, stop=True),
                                  reads=[t_p32, t_c1], writes=[t_px])
                            (r1, t_r1) = rt1.next()
                            (r2, t_r2) = rt2.next()
                            P.add(DVE, lambda e, r1=r1, p32=p32, csl=csl: e.tensor_tensor(out=r1[:], in0=p32[0:32, :], in1=cs[:, csl], op=ALU.mult),
                                  reads=[t_p32, t_cs], writes=[t_r1])
                            P.add(DVE, lambda e, r2=r2, px=px, csl=csl: e.tensor_tensor(out=r2[:], in0=px[0:32, :], in1=sn[:, csl], op=ALU.mult),
                                  reads=[t_px, t_sn], writes=[t_r2])
                            P.add(DVE, lambda e, r1=r1, r2=r2, qs=qs, csl=csl: e.tensor_tensor(out=qs[0:32, csl], in0=r1[:], in1=r2[:], op=ALU.add),
                                  reads=[t_r1, t_r2], writes=[t_qs])
                        hidx = int(nm[2:])
                        dst = S["QA"] if kind == "qa" else S["KA"]
                        store(dst[hidx, :, tb0:tb0 + BLK1], qs[:], [t_qs])
                    if kind == "dn":
                        cidx = int(nm[2:])
                        (px, t_px) = paux.next()
                        for kc in range(8):
                            P.add(PE, lambda e, px=px, wb=wb, kc=kc: e.matmul(out=px[:, 0:4], lhsT=wb[:, kc, 0:128], rhs=hT[:, kc, BLK1:BLK1 + 4],
                                                                        start=(kc == 0), stop=(kc == 7)),
                                  reads=[t_wb, t_hT], writes=[t_px])
                        P.add(ACT, lambda e, px=px, pq=pq: e.activation(out=pq[:, 0:2], in_=px[:, 0:2], func=AF.Copy), reads=[t_px], writes=[t_pq])
                        P.add(ACT, lambda e, px=px, pq=pq: e.activation(out=pq[:, BLK1 + 2:BLK1 + 4], in_=px[:, 2:4], func=AF.Copy), reads=[t_px], writes=[t_pq])
                        yield
                        (ac, t_ac) = acc.next()
                        P.add(DVE, lambda e, ac=ac, pq=pq, cidx=cidx: e.tensor_scalar(out=ac[:], in0=pq[:, 0:BLK1], scalar1=cw[:, cidx:cidx + 1], scalar2=None, op0=ALU.mult),
                              reads=[t_pq, t_c1], writes=[t_ac])
                        for j in range(1, 5):
                            P.add(DVE, lambda e, ac=ac, pq=pq, cidx=cidx, j=j: e.scalar_tensor_tensor(out=ac[:], in0=pq[:, j:j + BLK1], scalar=cw[:, j * 24 + cidx:j * 24 + cidx + 1], in1=ac[:],
                                                                                                   op0=ALU.mult, op1=ALU.add),
                                  reads=[t_pq, t_c1, t_ac], writes=[t_ac])
                        P.add(ACT, lambda e, ac=ac: e.activation(out=ac[:], in_=ac[:], func=AF.Silu), reads=[t_ac], writes=[t_ac])
                        if cidx >= 16:
                            hidx = cidx - 16
                            (vs, t_vs) = vst.next()
                            for i0 in range(0, NTB, 4):
                                (pt, t_pt) = ptr.next()
                                for j in range(4):
                                    P.add(PE, lambda e, pt=pt, ac=ac, i0=i0, j=j: e.transpose(out=pt[:, j, :], in_=ac[:, (i0 + j) * 128:(i0 + j + 1) * 128], identity=C.ident[:]),
                                          reads=[t_ac, C.t_const], writes=[t_pt])
                                P.add(ACT, lambda e, pt=pt, vs=vs, i0=i0: e.activation(out=vs[:, i0:i0 + 4, :], in_=pt[:], func=AF.Copy), reads=[t_pt], writes=[t_vs])
                            store(S["DV"][tb0:tb0 + BLK1, hidx * 128:(hidx + 1) * 128].rearrange("(i p) c -> p i c", p=128), vs[:], [t_vs])
                        else:
                            isq = cidx < 8
                            hidx = cidx % 8
                            (sq, t_sq) = sqb.next()
                            P.add(DVE, lambda e, sq=sq, ac=ac: e.tensor_tensor(out=sq[:], in0=ac[:], in1=ac[:], op=ALU.mult), reads=[t_ac], writes=[t_sq])
                            (ns, t_ns) = nst.next()
                            for mt in range(NMT):
                                csl = slice(mt * 512, (mt + 1) * 512)
                                (px, t_px) = paux.next()
                                P.add(PE, lambda e, px=px, sq=sq, csl=csl: e.matmul(out=px[:], lhsT=C.onesb[:], rhs=sq[:, csl], start=True, stop=True),
                                      reads=[t_sq, C.t_const], writes=[t_px])
                                (sdd, t_sd) = sd.next()
                                P.add(ACT, lambda e, px=px, sdd=sdd: e.activation(out=sdd[:], in_=px[:], func=AF.Sqrt, bias=C.eps[:], scale=1.0),
                                      reads=[t_px, C.t_const], writes=[t_sd])
                                P.add(DVE, lambda e, sdd=sdd: e.reciprocal(out=sdd[:], in_=sdd[:]), reads=[t_sd], writes=[t_sd])
                                qsc = float(HD ** -0.5) if isq else 1.0
                                P.add(DVE, lambda e, ns=ns, ac=ac, sdd=sdd, csl=csl, qsc=qsc: e.scalar_tensor_tensor(out=ns[:, csl], in0=ac[:, csl], scalar=qsc, in1=sdd[:],
                                                                                                               op0=ALU.mult, op1=ALU.mult),
                                      reads=[t_ac, t_sd], writes=[t_ns])
                            store((S["DQ"] if isq else S["DK"])[hidx, :, tb0:tb0 + BLK1], ns[:], [t_ns])
                            if not isq:
                                (ks, t_ks) = kst.next()
                                for i0 in range(0, NTB, 4):
                                    (pt, t_pt) = ptb.next()
                                    for j in range(4):
                                        P.add(PE, lambda e, pt=pt, ns=ns, i0=i0, j=j: e.transpose(out=pt[:, j, :], in_=ns[:, (i0 + j) * 128:(i0 + j + 1) * 128], identity=C.identb[:]),
                                              reads=[t_ns, C.t_const], writes=[t_pt])
                                    P.add(ACT, lambda e, pt=pt, ks=ks, i0=i0: e.activation(out=ks[:, i0:i0 + 4, :], in_=pt[:, 0:4, :], func=AF.Copy), reads=[t_pt], writes=[t_ks])
                                store(S["DKt"][tb0:tb0 + BLK1, hidx * 128:(hidx + 1) * 128].rearrange("(i p) c -> p i c", p=128), ks[:], [t_ks])
                else:
                    for i in range(NTB):
                        (pm, t_pm) = pmain.next()
                        r0 = tb0 + i * 128
                        for kc in range(8):
                            P.add(PE, lambda e, pm=pm, wb=wb, kc=kc, i=i, ncol=ncol: e.matmul(out=pm[:, 0:ncol], lhsT=hT[:, kc, i * 128:(i + 1) * 128], rhs=wb[:, kc, 0:ncol],
                                                                                        start=(kc == 0), stop=(kc == 7)),
                                  reads=[t_wb, t_hT], writes=[t_pm])
                        if nm == "va":
                            (vs, t_vs) = vast.next()
                            P.add(ACT, lambda e, pm=pm, vs=vs: e.activation(out=vs[:], in_=pm[:, 0:256], func=AF.Copy), reads=[t_pm], writes=[t_vs])
                            store(S["VA"][r0:r0 + 128, :], vs[:], [t_vs])
                        elif nm in ("z0", "z1"):
                            zc = 0 if nm == "z0" else 512
                            (zs, t_zs) = zst.next()
                            P.add(ACT, lambda e, pm=pm, zs=zs: e.activation(out=zs[:], in_=pm[:], func=AF.Silu), reads=[t_pm], writes=[t_zs])
                            store(S["ZS"][r0:r0 + 128, zc:zc + 512], zs[:], [t_zs])
                        else:
                            (at, t_at) = abt.next()
                            (ao, t_ao) = abo.next()
                            P.add(ACT, lambda e, pm=pm, at=at: e.activation(out=at[:, 0:32], in_=pm[:, 0:32], func=AF.Copy), reads=[t_pm], writes=[t_at])
                            P.add(ACT, lambda e, at=at, ao=ao: e.activation(out=ao[:, 16:32], in_=at[:, 16:32], func=AF.Sigmoid), reads=[t_at], writes=[t_ao])
                            P.add(DVE, lambda e, at=at: e.tensor_tensor(out=at[:, 0:16], in0=at[:, 0:16], in1=dtb[:], op=ALU.add), reads=[t_at, t_c1], writes=[t_at])
                            P.add(DVE, lambda e, at=at: e.tensor_scalar(out=at[:, 16:32], in0=at[:, 0:16], scalar1=-1.0, scalar2=None, op0=ALU.mult), reads=[t_at], writes=[t_at])
                            P.add(DVE, lambda e, at=at: e.tensor_tensor(out=at[:, 16:32], in0=at[:, 16:32], in1=at[:, 0:16], op=ALU.max), reads=[t_at], writes=[t_at])
                            P.add(ACT, lambda e, at=at: e.activation(out=at[:, 16:32], in_=at[:, 16:32], func=AF.Exp, scale=-1.0), reads=[t_at], writes=[t_at])
                            P.add(ACT, lambda e, at=at: e.activation(out=at[:, 16:32], in_=at[:, 16:32], func=AF.Ln, bias=1.0, scale=1.0), reads=[t_at], writes=[t_at])
                            P.add(DVE, lambda e, at=at: e.scalar_tensor_tensor(out=at[:, 0:16], in0=at[:, 0:16], scalar=0.0, in1=at[:, 16:32], op0=ALU.max, op1=ALU.add), reads=[t_at], writes=[t_at])
                            P.add(DVE, lambda e, at=at, ao=ao: e.tensor_tensor(out=ao[:, 0:16], in0=at[:, 0:16], in1=nA[:], op=ALU.mult), reads=[t_at, t_c1], writes=[t_ao])
                            store(S["AB"][r0:r0 + 128, :], ao[:], [t_ao])
            pend = []

            def exhaust(gcur):
                for _ in gcur:
                    pass
            dn_b = [bb for bb in blocks if bb[0].startswith("dn")]
            ot_b = [bb for bb in blocks if not bb[0].startswith("dn")]
            order = []
            while dn_b or ot_b:
                if ot_b:
                    order.append(ot_b.pop(0))
                if dn_b:
                    order.append(dn_b.pop(0))
            for (nm, c0, ncol) in order:
                gcur = wblock(nm, c0, ncol)
                try:
                    next(gcur)
                    pend.append(gcur)
                except StopIteration:
                    pass
                while len(pend) > 5:
                    exhaust(pend.pop(0))
            for gcur in pend:
                exhaust(gcur)
        K.bar1 = phase_barrier(K, ["st%d" % i for i in range(4)])
        K.finals = [t.w for t in K.bar1]


def phase_barrier(K, keys):
    toks = []
    for k in keys:
        op = K.P.dma_last.get(k)
        if op is not None:
            t = Tok("bar_" + k)
            t.w = op
            toks.append(t)
    return toks


def phase2(K):
    nc, P, I, S, C = K.nc, K.P, K.I, K.S, K.C
    T, SEG = K.T, K.SEG
    NB = T // 128
    BPS = SEG // 128
    bar = list(getattr(K, "bar1", []))
    scale = float(HD ** -0.5)
    with contextlib.ExitStack() as st:
        sb = lambda n, s, d: st.enter_context(nc.sbuf_tensor("p2_" + n, list(s), d))
        ps = lambda n, s, d: st.enter_context(nc.psum_tensor("p2_" + n, list(s), d))

        def mk(n, s, d, k=1, f=sb):
            return Ring([(f("%s%d" % (n, i), s, d), Tok("%s%d" % (n, i))) for i in range(k)])

        amask = sb("amask", [128, 5, 384], F32)
        sink = sb("sink", [128, 8], F32)
        kzero = sb("kzero", [128, 2, 128], BF16)
        vzero = sb("vzero", [128, 256], BF16)
        t_c2 = Tok("p2const")
        P.add(SP, lambda e: e.dma_start(out=amask[:], in_=I["amask"]), writes=[t_c2], dma_key="c0")
        P.add(SP, lambda e: e.dma_start(out=sink[:], in_=I["attn_sink"].partition_broadcast(128)), writes=[t_c2], dma_key="c1")
        P.add(DVE, lambda e: e.memset(kzero[:], 0.0), writes=[t_c2])
        P.add(DVE, lambda e: e.memset(vzero[:], 0.0), writes=[t_c2])
        kslots = [(sb("ks%d" % i, [128, 2, 128], BF16), Tok()) for i in range(4)]
        vslots = [(sb("vs%d" % i, [128, 256], BF16), Tok()) for i in range(4)]
        qblk = mk("qb", [128, 8, 128], BF16, 2)
        pss = mk("pss", [128, 512], F32, 2, ps)
        ptp = mk("ptp", [128, 8, 128], BF16, 2, ps)
        pop = mk("pop", [128, 4, 128], F32, 2, ps)
        smr = mk("sm", [128, 384], F32, 4)
        pur = mk("pu", [128, 384], F32, 2)
        pnr = mk("pn", [128, 384], BF16, 4)
        ptsr = mk("pts", [128, 3, 128], BF16, 4)
        str_ = mk("stt", [128, 8], F32, 8)
        aor = mk("ao", [128, 8, 128], BF16, 2)
        nst = [0]

        def load_kv(j):
            (ks, t_ks) = kslots[j % 4]
            (vs, t_vs) = vslots[j % 4]
            P.add(SP, lambda e, ks=ks, j=j: e.dma_start(out=ks[:], in_=S["KA"][:, :, j * 128:(j + 1) * 128].rearrange("g p t -> p g t")), reads=bar, writes=[t_ks],
                  dma_key="p2k%d" % (j % 4))
            P.add(SP, lambda e, vs=vs, j=j: e.dma_start(out=vs[:], in_=S["VA"][j * 128:(j + 1) * 128, :]), reads=bar, writes=[t_vs], dma_key="p2v%d" % (j % 4))

        def unit(qb, h, qt, t_qt, kind, slots, ao, t_ao, pobox):
            g = h // 4
            (pS, t_pS) = pss.next()
            for jj in range(3):
                (ks, t_ks) = slots[jj][0]
                P.add(PE, lambda e, ks=ks, jj=jj: e.matmul(out=pS[:, jj * 128:(jj + 1) * 128], lhsT=qt[:, h, :], rhs=ks[:, g, :], start=True, stop=True),
                      reads=[t_qt, t_ks], writes=[t_pS])
            (sm, t_sm) = smr.next()
            (stt, t_st) = str_.next()
            P.add(DVE, lambda e: e.scalar_tensor_tensor(out=sm[:], in0=pS[:, 0:384], scalar=scale, in1=amask[:, kind, :], op0=ALU.mult, op1=ALU.add),
                  reads=[t_pS, t_c2], writes=[t_sm])
            P.add(DVE, lambda e: e.tensor_reduce(out=stt[:, 0:1], in_=sm[:], axis=AX.X, op=ALU.max), reads=[t_sm], writes=[t_st])
            P.add(DVE, lambda e: e.tensor_scalar(out=stt[:, 1:2], in0=stt[:, 0:1], scalar1=sink[:, h:h + 1], scalar2=-1.0, op0=ALU.max, op1=ALU.mult),
                  reads=[t_st, t_c2], writes=[t_st])
            yield
            (pu, t_pu) = pur.next()
            P.add(ACT, lambda e: e.activation(out=pu[:], in_=sm[:], func=AF.Exp, bias=stt[:, 1:2], scale=1.0, accum_out=stt[:, 2:3]),
                  reads=[t_sm, t_st], writes=[t_pu, t_st])
            P.add(ACT, lambda e: e.activation(out=stt[:, 3:4], in_=sink[:, h:h + 1], func=AF.Exp, bias=stt[:, 1:2], scale=1.0),
                  reads=[t_st, t_c2], writes=[t_st])
            P.add(DVE, lambda e: e.tensor_tensor(out=stt[:, 4:5], in0=stt[:, 2:3], in1=stt[:, 3:4], op=ALU.add), reads=[t_st], writes=[t_st])
            P.add(DVE, lambda e: e.reciprocal(out=stt[:, 5:6], in_=stt[:, 4:5]), reads=[t_st], writes=[t_st])
            (pn, t_pn) = pnr.next()
            P.add(DVE, lambda e: e.tensor_scalar(out=pn[:], in0=pu[:], scalar1=stt[:, 5:6], scalar2=None, op0=ALU.mult), reads=[t_pu, t_st], writes=[t_pn])
            yield
            (pt, t_pt) = ptp.next()
            for jj in range(3):
                P.add(PE, lambda e, jj=jj: e.transpose(out=pt[:, jj, :], in_=pn[:, jj * 128:(jj + 1) * 128], identity=C.identb[:]),
                      reads=[t_pn, C.t_const], writes=[t_pt])
            (pts, t_pts) = ptsr.next()
            P.add(ACT, lambda e: e.activation(out=pts[:], in_=pt[:, 0:3, :], func=AF.Copy), reads=[t_pt], writes=[t_pts])
            yield
            if h % 4 == 0:
                pobox[0] = pop.next()
            (po, t_po) = pobox[0]
            for jj in range(3):
                (vs, t_vs) = slots[jj][1]
                P.add(PE, lambda e, vs=vs, jj=jj: e.matmul(out=po[:, h % 4, :], lhsT=vs[:, g * 128:(g + 1) * 128], rhs=pts[:, jj, :], start=(jj == 0), stop=(jj == 2)),
                      reads=[t_vs, t_pts], writes=[t_po])
            if h % 4 == 3:
                P.add(ACT, lambda e: e.activation(out=ao[:, g * 4:(g + 1) * 4, :], in_=po[:], func=AF.Copy), reads=[t_po], writes=[t_ao])
            if h == 7:
                k = "st%d" % (nst[0] % 4)
                nst[0] += 1
                P.add(POOL, lambda e: e.dma_start(out=S["AT"][:, :, qb * 128:(qb + 1) * 128].rearrange("h p t -> p h t"), in_=ao[:]), reads=[t_ao], dma_key=k)

        def unit_iter():
            load_kv(0)
            for qb in range(NB):
                if qb + 1 < NB:
                    load_kv(qb + 1)
                (qt, t_qt) = qblk.next()
                P.add(SP, lambda e, qt=qt, qb=qb: e.dma_start(out=qt[:], in_=S["QA"][:, :, qb * 128:(qb + 1) * 128].rearrange("h p t -> p h t")), reads=bar, writes=[t_qt],
                      dma_key="p2q%d" % ((qblk.i - 1) % 2))
                if qb == 0:
                    kind = 3
                elif qb == NB - 1:
                    kind = 4
                elif qb % BPS == 0:
                    kind = 1
                elif (qb + 1) % BPS == 0:
                    kind = 2
                else:
                    kind = 0
                slots = []
                for j in (qb - 1, qb, qb + 1):
                    if j < 0 or j >= NB:
                        slots.append(((kzero, t_c2), (vzero, t_c2)))
                    else:
                        slots.append((kslots[j % 4], vslots[j % 4]))
                (ao, t_ao) = aor.next()
                pobox = [None]
                for h in range(8):
                    yield unit(qb, h, qt, t_qt, kind, slots, ao, t_ao, pobox)

        W2 = 3
        it2 = unit_iter()
        active = []
        done_iter = False
        while True:
            while not done_iter and len(active) < W2:
                try:
                    active.append(next(it2))
                except StopIteration:
                    done_iter = True
            if not active:
                break
            for gcur in list(active):
                try:
                    next(gcur)
                except StopIteration:
                    active.remove(gcur)
        K.bar2 = phase_barrier(K, ["st%d" % i for i in range(4)])
        K.finals = [t.w for t in K.bar2]


def phase3(K):
    nc, P, I, S, C = K.nc, K.P, K.I, K.S, K.C
    T, SEG = K.T, K.SEG
    NT = T // 128
    BPS = SEG // 128
    bar = list(getattr(K, "bar1", []))
    with contextlib.ExitStack() as st:
        sb = lambda n, s, d: st.enter_context(nc.sbuf_tensor("p3_" + n, list(s), d))
        ps = lambda n, s, d: st.enter_context(nc.psum_tensor("p3_" + n, list(s), d))

        def mk(n, s, d, k=1):
            return Ring([(sb("%s%d" % (n, i), s, d), Tok("%s%d" % (n, i))) for i in range(k)])

        tri = sb("tri", [128, 6, 128], F32)
        t_c3 = Tok("p3const")
        P.add(SP, lambda e: e.dma_start(out=tri[:], in_=I["tri"]), writes=[t_c3], dma_key="c0")
        banks = [ps("bank%d" % i, [128, 4, 128], F32) for i in range(8)]
        btok = [Tok("bank%d" % i) for i in range(8)]
        BK = lambda b: (banks[b], btok[b])
        R_AB = Ring([(BK(0), BK(1)), (BK(5), BK(6))])
        bkX, bkXT = BK(2), BK(3)
        bkU = BK(4)
        bkR = BK(7)
        R_pG = Ring([(banks[7][:, 0, :], btok[7])])
        S32 = sb("S32", [128, 16, 128], F32)
        Sbf = sb("Sbf", [128, 16, 128], BF16)
        t_S = [Tok("S%d" % u) for u in range(4)]
        P.add(DVE, lambda e: e.memset(S32[:], 0.0), writes=t_S)
        P.add(DVE, lambda e: e.memset(Sbf[:], 0.0), writes=t_S)
        def dp(n, s, d, k=3):
            return [[(sb("%s_%d_%d" % (n, dd, p), s, d), Tok()) for p in range(k)] for dd in range(2)]
        B_ab = dp("ab", [128, 32], F32)
        B_gs = dp("gs", [128, 96], F32)
        B_kT = dp("kT", [128, 8, 128], BF16)
        B_qT = dp("qT", [128, 8, 128], BF16)
        B_V = dp("V", [128, 1024], BF16)
        R_Kt = mk("Kt", [128, 8, 128], BF16, 2)
        B_kd = dp("kd", [128, 8, 128], BF16)
        B_od = dp("od", [128, 1024], F32, 2)
        B_Tb = [[(sb("Tb_%d_%d" % (u, p), [128, 4, 128], BF16), Tok()) for p in range(2)] for u in range(4)]
        B_QK = [[(sb("QK_%d_%d" % (u, p), [128, 4, 128], BF16), Tok()) for p in range(2)] for u in range(4)]
        G4 = [128, 4, 128]
        R_rbA = mk("rbA", G4, F32, 3)
        R_rbB = mk("rbB", G4, F32, 3)
        R_t1 = mk("t1", G4, F32, 3)
        R_t2 = mk("t2", G4, F32, 3)
        R_AT = mk("AT", G4, F32, 4)
        R_A = mk("A", G4, F32, 4)
        R_X = mk("X", G4, F32, 6)
        R_XT = mk("XT", G4, F32, 6)
        R_Pt = mk("Pt", G4, F32, 4)
        R_Rt = mk("Rt", G4, F32, 2)
        R_Rp = mk("Rp", G4, BF16, 4)
        R_vn = mk("vn", G4, BF16, 4)
        R_o1 = mk("o1s", G4, F32, 4)
        R_tS = mk("tS", G4, F32, 2)
        nst = [0]

        def tile_of(n, d):
            return n if d == 0 else NT - 1 - n

        def prep(n):
            par = n % 3
            for d in range(2):
                it = tile_of(n, d)
                r0 = it * 128
                (ab, t_ab) = B_ab[d][par]; (gs, t_gs) = B_gs[d][par]; (kT, t_kT) = B_kT[d][par]; (qT, t_qT) = B_qT[d][par]
                (V, t_V) = B_V[d][par]; (Kt, t_Kt) = R_Kt.next(); (kd, t_kd) = B_kd[d][par]
                tag = "%d%d" % (d, par)
                P.add(SP, lambda e, ab=ab, r0=r0: e.dma_start(out=ab[:], in_=S["AB"][r0:r0 + 128, :]), reads=bar, writes=[t_ab], dma_key="p3ab" + tag)
                P.add(SP, lambda e, kT=kT, r0=r0: e.dma_start(out=kT[:], in_=S["DK"][:, :, r0:r0 + 128].rearrange("h p t -> p h t")), reads=bar, writes=[t_kT], dma_key="p3kT" + tag)
                P.add(SP, lambda e, qT=qT, r0=r0: e.dma_start(out=qT[:], in_=S["DQ"][:, :, r0:r0 + 128].rearrange("h p t -> p h t")), reads=bar, writes=[t_qT], dma_key="p3qT" + tag)
                P.add(SP, lambda e, V=V, r0=r0: e.dma_start(out=V[:], in_=S["DV"][r0:r0 + 128, :]), reads=bar, writes=[t_V], dma_key="p3V" + tag)
                P.add(SP, lambda e, Kt=Kt, r0=r0: e.dma_start(out=Kt[:].rearrange("p h c -> p (h c)"), in_=S["DKt"][r0:r0 + 128, :]), reads=bar, writes=[t_Kt], dma_key="p3Kt%d" % ((R_Kt.i - 1) % 2))
                g = ab[:, d * 8:(d + 1) * 8]
                beta = ab[:, 16 + d * 8:16 + (d + 1) * 8]
                (pG, t_pG) = R_pG.next()
                P.add(PE, lambda e, pG=pG, d=d, g=g: e.matmul(out=pG[:, 0:8], lhsT=tri[:, d, :], rhs=g, start=True, stop=True), reads=[t_c3, t_ab], writes=[t_pG])
                P.add(PE, lambda e, pG=pG, g=g: e.matmul(out=pG[:, 8:16], lhsT=C.onesf[:], rhs=g, start=True, stop=True), reads=[C.t_const, t_ab], writes=[t_pG])
                P.add(DVE, lambda e, pG=pG, gs=gs: e.tensor_copy(out=gs[:, 0:16], in_=pG[:, 0:16]), reads=[t_pG], writes=[t_gs])
                P.add(ACT, lambda e, gs=gs: e.activation(out=gs[:, 16:24], in_=gs[:, 0:8], func=AF.Exp), reads=[t_gs], writes=[t_gs])
                P.add(DVE, lambda e, gs=gs: e.tensor_scalar(out=gs[:, 24:32], in0=gs[:, 16:24], scalar1=-1.0, scalar2=None, op0=ALU.mult), reads=[t_gs], writes=[t_gs])
                P.add(DVE, lambda e, gs=gs: e.tensor_tensor(out=gs[:, 80:88], in0=gs[:, 8:16], in1=gs[:, 0:8], op=ALU.subtract), reads=[t_gs], writes=[t_gs])
                P.add(ACT, lambda e, gs=gs: e.activation(out=gs[:, 32:40], in_=gs[:, 80:88], func=AF.Exp), reads=[t_gs], writes=[t_gs])
                P.add(ACT, lambda e, gs=gs: e.activation(out=gs[:, 40:48], in_=gs[:, 8:16], func=AF.Exp), reads=[t_gs], writes=[t_gs])
                P.add(DVE, lambda e, gs=gs: e.tensor_scalar(out=gs[:, 48:56], in0=gs[:, 0:8], scalar1=-1.0, scalar2=None, op0=ALU.mult), reads=[t_gs], writes=[t_gs])
                P.add(DVE, lambda e, gs=gs, beta=beta: e.tensor_scalar(out=gs[:, 64:72], in0=beta, scalar1=1e-30, scalar2=None, op0=ALU.max), reads=[t_ab, t_gs], writes=[t_gs])
                P.add(ACT, lambda e, gs=gs: e.activation(out=gs[:, 56:64], in_=gs[:, 64:72], func=AF.Ln), reads=[t_gs], writes=[t_gs])
                P.add(DVE, lambda e, kd=kd, Kt=Kt, gs=gs: e.tensor_tensor(out=kd[:], in0=Kt[:], in1=gs[:, 32:40].unsqueeze(2).broadcast_to([128, 8, 128]), op=ALU.mult),
                      reads=[t_Kt, t_gs], writes=[t_kd])

        def bc_h(ap2):
            return ap2.unsqueeze(2).broadcast_to(G4)

        def bc_m(ap2):
            return ap2.unsqueeze(1).broadcast_to(G4)

        def tcomp(n, d, hg):
            par = n % 3
            par2 = n % 2
            u = d * 2 + hg
            h0 = hg * 4
            (ab, t_ab) = B_ab[d][par]; (gs, t_gs) = B_gs[d][par]; (kT, t_kT) = B_kT[d][par]; (qT, t_qT) = B_qT[d][par]
            g4 = ab[:, d * 8 + h0:d * 8 + h0 + 4]
            beta4 = ab[:, 16 + d * 8 + h0:16 + d * 8 + h0 + 4]
            (rbA, t_rbA) = R_rbA.next(); (rbB, t_rbB) = R_rbB.next()
            P.add(DVE, lambda e: e.tensor_tensor(out=rbA[:], in0=bc_m(tri[:, d, :]), in1=bc_h(g4), op=ALU.mult), reads=[t_c3, t_ab], writes=[t_rbA])
            P.add(DVE, lambda e: e.tensor_tensor(out=rbB[:], in0=bc_m(C.ident[:]), in1=bc_h(gs[:, 56 + h0:60 + h0]), op=ALU.mult), reads=[C.t_const, t_gs], writes=[t_rbB])
            P.add(DVE, lambda e: e.tensor_tensor(out=rbB[:], in0=rbB[:], in1=rbA[:], op=ALU.add), reads=[t_rbB, t_rbA], writes=[t_rbB])
            ((pA_, t_pA_), (pB_, t_pB_)) = R_AB.next()
            P.add(PE, lambda e: e.matmul(out=pA_[:].rearrange("p a c -> p (a c)"), lhsT=C.onesf[:], rhs=rbA[:].rearrange("p a c -> p (a c)"), start=True, stop=True),
                  reads=[C.t_const, t_rbA], writes=[t_pA_])
            P.add(PE, lambda e: e.matmul(out=pB_[:].rearrange("p a c -> p (a c)"), lhsT=C.onesf[:], rhs=rbB[:].rearrange("p a c -> p (a c)"), start=True, stop=True),
                  reads=[C.t_const, t_rbB], writes=[t_pB_])
            yield
            (t1, t_t1) = R_t1.next(); (t2, t_t2) = R_t2.next()
            ms, mi = (3, 4) if d == 0 else (2, 5)
            negG4 = gs[:, 48 + h0:52 + h0]
            P.add(DVE, lambda e: e.tensor_tensor(out=t1[:], in0=pB_[:], in1=bc_h(negG4), op=ALU.add), reads=[t_pB_, t_gs], writes=[t_t1])
            P.add(DVE, lambda e: e.tensor_tensor(out=t1[:], in0=t1[:], in1=bc_m(tri[:, ms, :]), op=ALU.add), reads=[t_t1, t_c3], writes=[t_t1])
            P.add(ACT, lambda e: e.activation(out=t1[:], in_=t1[:], func=AF.Exp), reads=[t_t1], writes=[t_t1])
            P.add(DVE, lambda e: e.tensor_tensor(out=t2[:], in0=pA_[:], in1=bc_h(negG4), op=ALU.add), reads=[t_pA_, t_gs], writes=[t_t2])
            P.add(DVE, lambda e: e.tensor_tensor(out=t2[:], in0=t2[:], in1=bc_m(tri[:, mi, :]), op=ALU.add), reads=[t_t2, t_c3], writes=[t_t2])
            P.add(ACT, lambda e: e.activation(out=t2[:], in_=t2[:], func=AF.Exp), reads=[t_t2], writes=[t_t2])
            for j in range(4):
                P.add(PE, lambda e, j=j: e.matmul(out=pA_[:, j, :], lhsT=kT[:, h0 + j, :], rhs=kT[:, h0 + j, :], start=True, stop=True), reads=[t_kT], writes=[t_pA_])
            for j in range(4):
                P.add(PE, lambda e, j=j: e.matmul(out=pB_[:, j, :], lhsT=kT[:, h0 + j, :], rhs=qT[:, h0 + j, :], start=True, stop=True), reads=[t_kT, t_qT], writes=[t_pB_])
            yield
            (AT, t_AT) = R_AT.next()
            (QK, t_QK) = B_QK[u][par2]
            P.add(DVE, lambda e: e.tensor_tensor(out=AT[:].bitcast(F32R), in0=pA_[:], in1=t1[:], op=ALU.mult), reads=[t_pA_, t_t1], writes=[t_AT])
            P.add(DVE, lambda e: e.tensor_tensor(out=QK[:], in0=pB_[:], in1=t2[:], op=ALU.mult), reads=[t_pB_, t_t2], writes=[t_QK])
            yield
            (pX, t_pX) = bkX; (pXT, t_pXT) = bkXT; (pU, t_pU) = bkU
            (A, t_A) = R_A.next()
            (Pt, t_Pt) = R_Pt.next()
            for j in range(4):
                P.add(PE, lambda e, j=j: e.transpose(out=pX[:, j, :], in_=AT[:, j, :], identity=C.ident[:]), reads=[t_AT, C.t_const], writes=[t_pX])
            P.add(ACT, lambda e: e.activation(out=A[:].bitcast(F32R), in_=pX[:], func=AF.Copy), reads=[t_pX], writes=[t_A])
            P.add(DVE, lambda e: e.tensor_tensor(out=Pt[:].bitcast(F32R), in0=bc_m(C.ident[:]), in1=AT[:], op=ALU.subtract), reads=[C.t_const, t_AT], writes=[t_Pt])
            yield
            X, t_X, XT, t_XT = A, t_A, AT, t_AT
            prev = None

            def p_update(Xp, t_Xp):
                for j in range(4):
                    P.add(PE, lambda e, j=j: e.matmul(out=pU[:, j, :], lhsT=Xp[:, j, :].bitcast(F32R), rhs=Pt[:, j, :].bitcast(F32R), start=True, stop=True), reads=[t_Xp, t_Pt], writes=[t_pU])
                P.add(DVE, lambda e: e.tensor_tensor(out=Pt[:].bitcast(F32R), in0=pU[:], in1=Pt[:], op=ALU.add), reads=[t_pU, t_Pt], writes=[t_Pt])

            for k in range(1, 7):
                (Xn, t_Xn) = R_X.next()
                for j in range(4):
                    P.add(PE, lambda e, j=j, X=X, XT=XT: e.matmul(out=pX[:, j, :], lhsT=XT[:, j, :].bitcast(F32R), rhs=X[:, j, :].bitcast(F32R), start=True, stop=True), reads=[t_X, t_XT], writes=[t_pX])
                P.add(ACT, lambda e, Xn=Xn: e.activation(out=Xn[:].bitcast(F32R), in_=pX[:], func=AF.Copy), reads=[t_pX], writes=[t_Xn])
                if k < 6:
                    (XTn, t_XTn) = R_XT.next()
                    for j in range(4):
                        P.add(PE, lambda e, j=j, X=X, XT=XT: e.matmul(out=pXT[:, j, :], lhsT=X[:, j, :].bitcast(F32R), rhs=XT[:, j, :].bitcast(F32R), start=True, stop=True), reads=[t_X, t_XT], writes=[t_pXT])
                    P.add(ACT, lambda e, XTn=XTn: e.activation(out=XTn[:].bitcast(F32R), in_=pXT[:], func=AF.Copy), reads=[t_pXT], writes=[t_XTn])
                else:
                    XTn, t_XTn = None, None
                if prev is not None:
                    p_update(*prev)
                yield
                prev = (Xn, t_Xn)
                X, t_X, XT, t_XT = Xn, t_Xn, XTn, t_XTn
            p_update(*prev)
            (Tb, t_Tb) = B_Tb[u][par2]
            P.add(DVE, lambda e: e.tensor_tensor(out=Tb[:], in0=Pt[:], in1=bc_h(beta4), op=ALU.mult), reads=[t_Pt, t_ab], writes=[t_Tb])

        def rec(n, d, hg):
            par = n % 3
            par2 = n % 2
            u = d * 2 + hg
            h0 = hg * 4
            us = slice(d * 8 + h0, d * 8 + h0 + 4)
            (gs, t_gs) = B_gs[d][par]; (kT, t_kT) = B_kT[d][par]; (qT, t_qT) = B_qT[d][par]
            (V, t_V) = B_V[d][par]; (kd, t_kd) = B_kd[d][par]; (od, t_od) = B_od[d][par2]
            (Tb, t_Tb) = B_Tb[u][par2]; (QK, t_QK) = B_QK[u][par2]
            tS = t_S[u]
            (p1, t_p1) = bkR
            for j in range(4):
                P.add(PE, lambda e, j=j: e.matmul(out=p1[:, j, :], lhsT=kT[:, h0 + j, :], rhs=Sbf[:, d * 8 + h0 + j, :], start=True, stop=True), reads=[t_kT, tS], writes=[t_p1])
            (Rt, t_Rt) = R_Rt.next(); (Rp, t_Rp) = R_Rp.next(); (o1s, t_o1s) = R_o1.next()
            V4 = V[:, h0 * 128:(h0 + 4) * 128].rearrange("p (a c) -> p a c", a=4)
            P.add(DVE, lambda e: e.tensor_tensor(out=Rt[:], in0=p1[:], in1=bc_h(gs[:, 24 + h0:28 + h0]), op=ALU.mult), reads=[t_p1, t_gs], writes=[t_Rt])
            P.add(DVE, lambda e: e.tensor_tensor(out=Rp[:], in0=Rt[:], in1=V4, op=ALU.add), reads=[t_Rt, t_V], writes=[t_Rp])
            for j in range(4):
                P.add(PE, lambda e, j=j: e.matmul(out=p1[:, j, :], lhsT=qT[:, h0 + j, :], rhs=Sbf[:, d * 8 + h0 + j, :], start=True, stop=True), reads=[t_qT, tS], writes=[t_p1])
            P.add(DVE, lambda e: e.tensor_tensor(out=o1s[:], in0=p1[:], in1=bc_h(gs[:, 16 + h0:20 + h0]), op=ALU.mult), reads=[t_p1, t_gs], writes=[t_o1s])
            yield
            (vn, t_vn) = R_vn.next()
            for j in range(4):
                P.add(PE, lambda e, j=j: e.matmul(out=p1[:, j, :], lhsT=Tb[:, j, :], rhs=Rp[:, j, :], start=True, stop=True), reads=[t_Tb, t_Rp], writes=[t_p1])
            P.add(DVE, lambda e: e.tensor_copy(out=vn[:], in_=p1[:]), reads=[t_p1], writes=[t_vn])
            yield
            for j in range(4):
                P.add(PE, lambda e, j=j: e.matmul(out=p1[:, j, :], lhsT=QK[:, j, :], rhs=vn[:, j, :], start=True, stop=True), reads=[t_QK, t_vn], writes=[t_p1])
            od4 = od[:, h0 * 128:(h0 + 4) * 128].rearrange("p (a c) -> p a c", a=4)
            P.add(DVE, lambda e: e.tensor_tensor(out=od4, in0=p1[:], in1=o1s[:], op=ALU.add), reads=[t_p1, t_o1s], writes=[t_od])
            (tSb, t_tSb) = R_tS.next()
            P.add(DVE, lambda e: e.tensor_tensor(out=tSb[:], in0=S32[:, us, :], in1=bc_h(gs[:, 40 + h0:44 + h0]), op=ALU.mult), reads=[tS, t_gs], writes=[t_tSb])
            for j in range(4):
                P.add(PE, lambda e, j=j: e.matmul(out=p1[:, j, :], lhsT=kd[:, h0 + j, :], rhs=vn[:, j, :], start=True, stop=True), reads=[t_kd, t_vn], writes=[t_p1])
            P.add(DVE, lambda e: e.tensor_tensor(out=S32[:, us, :], in0=p1[:], in1=tSb[:], op=ALU.add), reads=[t_p1, t_tSb, tS], writes=[tS])
            yield
            P.add(ACT, lambda e: e.activation(out=Sbf[:, us, :], in_=S32[:, us, :], func=AF.Copy), reads=[tS], writes=[tS])

        def run_windows(groups):
            pending = [list(g) for g, _ in groups]
            active = [[] for _ in groups]
            while any(pending) or any(active):
                for gi, (_, w) in enumerate(groups):
                    while len(active[gi]) < w and pending[gi]:
                        active[gi].append(pending[gi].pop(0))
                for gi in range(len(groups)):
                    for gcur in list(active[gi]):
                        try:
                            next(gcur)
                        except StopIteration:
                            active[gi].remove(gcur)

        prep(0)
        for n in range(-1, NT):
            gens = []
            if n + 2 < NT:
                prep(n + 2)
            if n + 1 < NT:
                gens += [tcomp(n + 1, d, hg) for hg in range(2) for d in range(2)]
            if n >= 0:
                for d in range(2):
                    it = tile_of(n, d)
                    seg_start = (it % BPS == 0) if d == 0 else (it % BPS == BPS - 1)
                    if n > 0 and seg_start:
                        toks = t_S[d * 2:(d + 1) * 2]
                        P.add(DVE, lambda e, d=d: e.tensor_scalar(out=S32[:, d * 8:(d + 1) * 8, :], in0=S32[:, d * 8:(d + 1) * 8, :], scalar1=C.carry[:, 0:1], scalar2=None, op0=ALU.mult),
                              reads=toks + [C.t_const], writes=toks)
                        P.add(ACT, lambda e, d=d: e.activation(out=Sbf[:, d * 8:(d + 1) * 8, :], in_=S32[:, d * 8:(d + 1) * 8, :], func=AF.Copy), reads=toks, writes=toks)
                recs = [rec(n, d, hg) for hg in range(2) for d in range(2)]
            else:
                recs = []
            run_windows([(recs, int(os.environ.get("W_REC", "2"))), (gens, int(os.environ.get("W_TC", "2")))])
            if n >= 0:
                par = n % 2
                for d in range(2):
                    it = tile_of(n, d)
                    (od, t_od) = B_od[d][par]
                    k = "st%d" % (nst[0] % 4)
                    nst[0] += 1
                    dst = S["OF"] if d == 0 else S["OB"]
                    P.add(POOL, lambda e, od=od, it=it, dst=dst: e.dma_start(out=dst[it * 128:(it + 1) * 128, :], in_=od[:]), reads=[t_od], dma_key=k)
        K.bar3 = phase_barrier(K, ["st%d" % i for i in range(4)])
        K.finals = [t.w for t in K.bar3]


def phase4(K):
    nc, P, I, S, C = K.nc, K.P, K.I, K.S, K.C
    T = K.T
    NM = T // 512
    bar = list(getattr(K, "bar1", [])) + list(getattr(K, "bar2", [])) + list(getattr(K, "bar3", []))
    finals = []
    with contextlib.ExitStack() as st:
        sb = lambda n, s, d: st.enter_context(nc.sbuf_tensor("p4_" + n, list(s), d))
        ps = lambda n, s, d: st.enter_context(nc.psum_tensor("p4_" + n, list(s), d))

        def mk(n, s, d, k=1, f=sb):
            return Ring([(f("%s%d" % (n, i), s, d), Tok("%s%d" % (n, i))) for i in range(k)])

        ptr = mk("ptr", [128, 4, 128], F32, 2, ps)
        ptb = mk("ptb", [128, 8, 128], BF16, 1, ps)
        pmm = mk("pmm", [128, 512], F32, 4, ps)
        pstat = mk("pstat", [128, 512], F32, 1, ps)
        gvec = sb("gvec", [128, 24], F32)
        nw = sb("nw", [128, 128], F32)
        ldc = sb("ldc", [24, 128], F32)
        t_c4 = Tok("p4const")
        t_ldc = Tok()
        for gi, gname in enumerate(("g_post_mix", "g_pre_ffn", "g_post_ffn")):
            P.add(SP, lambda e, gi=gi, gname=gname: e.dma_start(out=ldc[gi * 8:(gi + 1) * 8, :], in_=I[gname].rearrange("(kc p) -> kc p", p=128)), writes=[t_ldc], dma_key="c%d" % (gi % 2))
        (pt0, t_pt0) = ptr.next()
        P.add(PE, lambda e: e.transpose(out=pt0[:, 0, 0:24], in_=ldc[0:24, :], identity=C.ident[0:24, 0:24]), reads=[t_ldc, C.t_const], writes=[t_pt0])
        P.add(ACT, lambda e: e.activation(out=gvec[:], in_=pt0[:, 0, 0:24], func=AF.Copy), reads=[t_pt0], writes=[t_c4])
        P.add(SP, lambda e: e.dma_start(out=nw[:], in_=I["dn_norm_w"].partition_broadcast(128)), writes=[t_c4], dma_key="c0")

        ofr = mk("of", [128, 1024], F32, 1)
        obr = mk("ob", [128, 1024], F32, 1)
        zsr = mk("zs", [128, 1024], F32, 1)
        sqtr = mk("sqt", [128, 1024], F32, 1)
        dnbr = mk("dnb", [128, 1024], BF16, 1)
        st8 = mk("st8", [128, 24], F32, 2)
        xinr = mk("xin", [128, 1024], F32, 2)
        yor = mk("yo", [128, 1024], F32, 2)
        aTr = mk("aT", [128, 8, 512], BF16, 1)
        dnTr = mk("dnT", [128, 8, 512], BF16, 1)
        xTr = mk("xT", [128, 8, 512], F32, 1)
        mixTr = mk("mixT", [128, 8, 512], BF16, 1)
        moTr = mk("moT", [128, 8, 512], F32, 1)
        h2Tr = mk("h2T", [128, 8, 512], BF16, 1)
        actTr = mk("actT", [128, NFF, 512], BF16, 1)
        wsm = mk("wsm", [128, 8, 128], BF16, 4)
        wdr = mk("wd", [128, NFF, 128], BF16, 2)
        gar = mk("ga", [128, 512], F32, 2)
        gdr = mk("gd", [128, 512], F32, 2)
        tr1 = mk("t1", [128, 512], F32, 1)
        tr2 = mk("t2", [128, 512], F32, 1)
        sgr = mk("sg", [128, 512], F32, 2)
        sqmr = mk("sqm", [128, 512], BF16, 2)
        rsr = mk("rs", [128, 512], F32, 2)
        tmr = mk("tm", [128, 512], F32, 2)
        nst = [0]

        def wload(ring, scr, off, nkc, tname):
            (w, t_w) = ring.next()
            idx = (ring.i - 1) % len(ring.bufs)
            src = S[scr][:, off:off + nkc * 128].rearrange("p (kc c) -> p kc c", kc=nkc)
            P.add(SP, lambda e, w=w, src=src: e.dma_start(out=w[:], in_=src), reads=[K.t_w[tname]], writes=[t_w], dma_key="p4w%s%d" % ("s" if nkc == 8 else "d", idx))
            return w, t_w

        def stat_norm(srcT, t_src, scale):
            (pst, t_pst) = pstat.next()
            for c in range(8):
                (sqm, t_sqm) = sqmr.next()
                P.add(ACT, lambda e, sqm=sqm, c=c: e.activation(out=sqm[:], in_=srcT[:, c, :], func=AF.Square), reads=[t_src], writes=[t_sqm])
                P.add(PE, lambda e, pst=pst, sqm=sqm, c=c: e.matmul(out=pst[:], lhsT=C.onesb[:], rhs=sqm[:], start=(c == 0), stop=(c == 7)), reads=[t_sqm, C.t_const], writes=[t_pst])
            (rs, t_rs) = rsr.next()
            P.add(ACT, lambda e, rs=rs, pst=pst: e.activation(out=rs[:], in_=pst[:], func=AF.Sqrt, bias=C.eps[:], scale=scale), reads=[t_pst, C.t_const], writes=[t_rs])
            P.add(DVE, lambda e, rs=rs: e.reciprocal(out=rs[:], in_=rs[:]), reads=[t_rs], writes=[t_rs])
            return rs, t_rs

        for m in range(NM):
            t0 = m * 512
            (aT, t_aT) = aTr.next(); (dnT, t_dnT) = dnTr.next(); (xT, t_xT) = xTr.next(); (mixT, t_mixT) = mixTr.next()
            (moT, t_moT) = moTr.next(); (h2T, t_h2T) = h2Tr.next(); (actT, t_actT) = actTr.next()
            P.add(SP, lambda e, aT=aT, t0=t0: e.dma_start(out=aT[:], in_=S["AT"][:, :, t0:t0 + 512].rearrange("h p t -> p h t")), reads=bar, writes=[t_aT], dma_key="p4aT")
            for i in range(4):
                r0 = t0 + i * 128
                (of, t_of) = ofr.next(); (ob, t_ob) = obr.next(); (zs, t_zs) = zsr.next(); (xt, t_xt) = xinr.next()
                P.add(SP, lambda e, of=of, r0=r0: e.dma_start(out=of[:], in_=S["OF"][r0:r0 + 128, :]), reads=bar, writes=[t_of], dma_key="p4of")
                P.add(SP, lambda e, ob=ob, r0=r0: e.dma_start(out=ob[:], in_=S["OB"][r0:r0 + 128, :]), reads=bar, writes=[t_ob], dma_key="p4ob")
                P.add(SP, lambda e, zs=zs, r0=r0: e.dma_start(out=zs[:], in_=S["ZS"][r0:r0 + 128, :]), reads=bar, writes=[t_zs], dma_key="p4zs")
                P.add(SP, lambda e, xt=xt, r0=r0: e.dma_start(out=xt[:], in_=I["x"][r0:r0 + 128, :]), writes=[t_xt], dma_key="p4x%d" % ((xinr.i - 1) % 2))
                P.add(DVE, lambda e, of=of, ob=ob: e.tensor_tensor(out=of[:], in0=of[:], in1=ob[:], op=ALU.add), reads=[t_of, t_ob], writes=[t_of])
                (sqt, t_sqt) = sqtr.next()
                (s8, t_s8) = st8.next()
                P.add(ACT, lambda e, sqt=sqt, of=of: e.activation(out=sqt[:], in_=of[:], func=AF.Square), reads=[t_of], writes=[t_sqt])
                P.add(DVE, lambda e, s8=s8, sqt=sqt: e.tensor_reduce(out=s8[:, 0:8], in_=sqt[:].rearrange("p (h c) -> p h c", h=8), axis=AX.X, op=ALU.add), reads=[t_sqt], writes=[t_s8])
                P.add(ACT, lambda e, s8=s8: e.activation(out=s8[:, 8:16], in_=s8[:, 0:8], func=AF.Sqrt, bias=C.eps[:], scale=1.0 / 128.0), reads=[t_s8, C.t_const], writes=[t_s8])
                P.add(DVE, lambda e, s8=s8: e.reciprocal(out=s8[:, 16:24], in_=s8[:, 8:16]), reads=[t_s8], writes=[t_s8])
                o3 = of[:].rearrange("p (h c) -> p h c", h=8)
                P.add(DVE, lambda e, o3=o3, s8=s8: e.tensor_tensor(out=o3, in0=o3, in1=s8[:, 16:24].unsqueeze(2).broadcast_to([128, 8, 128]), op=ALU.mult), reads=[t_of, t_s8], writes=[t_of])
                P.add(DVE, lambda e, o3=o3: e.tensor_tensor(out=o3, in0=o3, in1=nw[:].unsqueeze(1).broadcast_to([128, 8, 128]), op=ALU.mult), reads=[t_of, t_c4], writes=[t_of])
                (dnb, t_dnb) = dnbr.next()
                P.add(DVE, lambda e, dnb=dnb, of=of, zs=zs: e.tensor_tensor(out=dnb[:], in0=of[:], in1=zs[:], op=ALU.mult), reads=[t_of, t_zs], writes=[t_dnb])
                (pb, t_pb) = ptb.next()
                for h in range(8):
                    P.add(PE, lambda e, pb=pb, dnb=dnb, h=h: e.transpose(out=pb[:, h, :], in_=dnb[:, h * 128:(h + 1) * 128], identity=C.identb[:]), reads=[t_dnb, C.t_const], writes=[t_pb])
                P.add(ACT, lambda e, pb=pb, dnT=dnT, i=i: e.activation(out=dnT[:, :, i * 128:(i + 1) * 128], in_=pb[:], func=AF.Copy), reads=[t_pb], writes=[t_dnT])
                for half in range(2):
                    (pt, t_pt) = ptr.next()
                    for j in range(4):
                        kc = half * 4 + j
                        P.add(PE, lambda e, pt=pt, xt=xt, kc=kc, j=j: e.transpose(out=pt[:, j, :], in_=xt[:, kc * 128:(kc + 1) * 128], identity=C.ident[:]), reads=[t_xt, C.t_const], writes=[t_pt])
                    P.add(ACT, lambda e, pt=pt, xT=xT, half=half, i=i: e.activation(out=xT[:, half * 4:(half + 1) * 4, i * 128:(i + 1) * 128], in_=pt[:], func=AF.Copy), reads=[t_pt], writes=[t_xT])
            for c in range(8):
                (wa, t_wa) = wload(wsm, "Wb_ao", c * 1024, 8, "Wb_ao_%d" % c)
                (wd_, t_wd_) = wload(wsm, "Wb_do", c * 1024, 8, "Wb_do_%d" % c)
                (ga, t_ga) = gar.next(); (gd, t_gd) = gdr.next()
                P.add(SP, lambda e, ga=ga, c=c, t0=t0: e.dma_start(out=ga[:], in_=S["GT"][c, :, t0:t0 + 512]), reads=bar, writes=[t_ga], dma_key="p4ga%d" % ((gar.i - 1) % 2))
                P.add(SP, lambda e, gd=gd, c=c, t0=t0: e.dma_start(out=gd[:], in_=S["GT"][8 + c, :, t0:t0 + 512]), reads=bar, writes=[t_gd], dma_key="p4gd%d" % ((gdr.i - 1) % 2))
                (pa, t_pa) = pmm.next()
                for kc in range(8):
                    P.add(PE, lambda e, pa=pa, wa=wa, kc=kc: e.matmul(out=pa[:], lhsT=wa[:, kc, :], rhs=aT[:, kc, :], start=(kc == 0), stop=(kc == 7)), reads=[t_wa, t_aT], writes=[t_pa])
                (pd, t_pd) = pmm.next()
                for kc in range(8):
                    P.add(PE, lambda e, pd=pd, wd_=wd_, kc=kc: e.matmul(out=pd[:], lhsT=wd_[:, kc, :], rhs=dnT[:, kc, :], start=(kc == 0), stop=(kc == 7)), reads=[t_wd_, t_dnT], writes=[t_pd])
                (t1, t_t1) = tr1.next(); (t2, t_t2) = tr2.next()
                P.add(DVE, lambda e, t1=t1, pa=pa, ga=ga: e.tensor_tensor(out=t1[:], in0=pa[:], in1=ga[:], op=ALU.mult), reads=[t_pa, t_ga], writes=[t_t1])
                P.add(DVE, lambda e, t2=t2, pd=pd, gd=gd: e.tensor_tensor(out=t2[:], in0=pd[:], in1=gd[:], op=ALU.mult), reads=[t_pd, t_gd], writes=[t_t2])
                P.add(DVE, lambda e, t1=t1, t2=t2, c=c: e.tensor_tensor(out=mixT[:, c, :], in0=t1[:], in1=t2[:], op=ALU.add), reads=[t_t1, t_t2], writes=[t_mixT])
            for c in range(8):
                (wo, t_wo) = wload(wsm, "Wb_out", c * 1024, 8, "Wb_out_%d" % c)
                (pm, t_pm) = pmm.next()
                for kc in range(8):
                    P.add(PE, lambda e, pm=pm, wo=wo, kc=kc: e.matmul(out=pm[:], lhsT=wo[:, kc, :], rhs=mixT[:, kc, :], start=(kc == 0), stop=(kc == 7)), reads=[t_wo, t_mixT], writes=[t_pm])
                P.add(ACT, lambda e, pm=pm, c=c: e.activation(out=moT[:, c, :], in_=pm[:], func=AF.Copy), reads=[t_pm], writes=[t_moT])
            rs, t_rs = stat_norm(moT, t_moT, 1.0 / D_MODEL)
            for c in range(8):
                (tm, t_tm) = tmr.next()
                P.add(DVE, lambda e, tm=tm, c=c, rs=rs: e.tensor_tensor(out=tm[:], in0=moT[:, c, :], in1=rs[:], op=ALU.mult), reads=[t_moT, t_rs], writes=[t_tm])
                P.add(DVE, lambda e, tm=tm, c=c: e.scalar_tensor_tensor(out=xT[:, c, :], in0=tm[:], scalar=gvec[:, c:c + 1], in1=xT[:, c, :], op0=ALU.mult, op1=ALU.add),
                      reads=[t_tm, t_c4, t_xT], writes=[t_xT])
            rs2, t_rs2 = stat_norm(xT, t_xT, 1.0 / D_MODEL)
            for c in range(8):
                P.add(DVE, lambda e, c=c, rs2=rs2: e.scalar_tensor_tensor(out=h2T[:, c, :], in0=xT[:, c, :], scalar=gvec[:, 8 + c:9 + c], in1=rs2[:], op0=ALU.mult, op1=ALU.mult),
                      reads=[t_xT, t_c4, t_rs2], writes=[t_h2T])
            for f in range(NFF):
                (wg, t_wg) = wload(wsm, "Wb_g", f * 1024, 8, "Wb_g_%d" % f)
                (wu, t_wu) = wload(wsm, "Wb_u", f * 1024, 8, "Wb_u_%d" % f)
                (pg, t_pg) = pmm.next()
                for kc in range(8):
                    P.add(PE, lambda e, pg=pg, wg=wg, kc=kc: e.matmul(out=pg[:], lhsT=wg[:, kc, :], rhs=h2T[:, kc, :], start=(kc == 0), stop=(kc == 7)), reads=[t_wg, t_h2T], writes=[t_pg])
                (pu, t_pu) = pmm.next()
                for kc in range(8):
                    P.add(PE, lambda e, pu=pu, wu=wu, kc=kc: e.matmul(out=pu[:], lhsT=wu[:, kc, :], rhs=h2T[:, kc, :], start=(kc == 0), stop=(kc == 7)), reads=[t_wu, t_h2T], writes=[t_pu])
                (sg, t_sg) = sgr.next()
                P.add(ACT, lambda e, sg=sg, pg=pg: e.activation(out=sg[:], in_=pg[:], func=AF.Silu), reads=[t_pg], writes=[t_sg])
                P.add(DVE, lambda e, sg=sg, pu=pu, f=f: e.tensor_tensor(out=actT[:, f, :], in0=pu[:], in1=sg[:], op=ALU.mult), reads=[t_pu, t_sg], writes=[t_actT])
            for c in range(8):
                (wdn, t_wdn) = wload(wdr, "Wb_d", c * NFF * 128, NFF, "Wb_d_%d" % c)
                (pf, t_pf) = pmm.next()
                for f in range(NFF):
                    P.add(PE, lambda e, pf=pf, wdn=wdn, f=f: e.matmul(out=pf[:], lhsT=wdn[:, f, :], rhs=actT[:, f, :], start=(f == 0), stop=(f == NFF - 1)), reads=[t_wdn, t_actT], writes=[t_pf])
                P.add(ACT, lambda e, pf=pf, c=c: e.activation(out=moT[:, c, :], in_=pf[:], func=AF.Copy), reads=[t_pf], writes=[t_moT])
            rs3, t_rs3 = stat_norm(moT, t_moT, 1.0 / D_MODEL)
            for c in range(8):
                (tm, t_tm) = tmr.next()
                P.add(DVE, lambda e, tm=tm, c=c, rs3=rs3: e.tensor_tensor(out=tm[:], in0=moT[:, c, :], in1=rs3[:], op=ALU.mult), reads=[t_moT, t_rs3], writes=[t_tm])
                P.add(DVE, lambda e, tm=tm, c=c: e.scalar_tensor_tensor(out=moT[:, c, :], in0=tm[:], scalar=gvec[:, 16 + c:17 + c], in1=xT[:, c, :], op0=ALU.mult, op1=ALU.add),
                      reads=[t_tm, t_c4, t_xT, t_moT], writes=[t_moT])
            for i in range(4):
                r0 = t0 + i * 128
                (yo, t_yo) = yor.next()
                for half in range(2):
                    (pt, t_pt) = ptr.next()
                    for j in range(4):
                        kc = half * 4 + j
                        P.add(PE, lambda e, pt=pt, kc=kc, j=j, i=i: e.transpose(out=pt[:, j, :], in_=moT[:, kc, i * 128:(i + 1) * 128], identity=C.ident[:]), reads=[t_moT, C.t_const], writes=[t_pt])
                    P.add(ACT, lambda e, pt=pt, yo=yo, half=half: e.activation(out=yo[:, half * 512:(half + 1) * 512], in_=pt[:].rearrange("p a c -> p (a c)"), func=AF.Copy), reads=[t_pt], writes=[t_yo])
                k = "yst%d" % (nst[0] % 4)
                nst[0] += 1
                o = P.add(POOL, lambda e, yo=yo, r0=r0: e.dma_start(out=K.y[r0:r0 + 128, :], in_=yo[:]), reads=[t_yo], dma_key=k)
        finals = [K.P.dma_last[k] for k in ("yst0", "yst1", "yst2", "yst3") if k in K.P.dma_last]
    return finals


def host_consts(T, SEG, BLK1, carry):
    c = {}
    c["ident"] = np.eye(128, dtype=np.float32)
    rm = np.zeros((128, 128), np.float32)
    for d in range(16):
        rm[d + 16, d] = -1.0
        rm[d, d + 16] = 1.0
    c["rotm"] = rm
    t = np.arange(T)
    pos = (t if carry else (t % SEG)).astype(np.float32)
    inv = (500000.0 ** (-np.arange(16, dtype=np.float32) * 2.0 / 32.0)).astype(np.float32)
    ang = pos[None, :] * inv[:, None]
    ang = np.concatenate([ang, ang], axis=0)
    c["cosT"] = np.cos(ang).astype(np.float32)
    c["sinT"] = np.sin(ang).astype(np.float32)
    NB1 = T // BLK1
    hf = np.zeros((4, NB1), np.float32)
    for b in range(NB1):
        tb0 = b * BLK1
        l = 0.0 if tb0 == 0 else (float(carry) if tb0 % SEG == 0 else 1.0)
        e = tb0 + BLK1
        r = 0.0 if e == T else (float(carry) if e % SEG == 0 else 1.0)
        hf[0:2, b] = l
        hf[2:4, b] = r
    c["hflags"] = hf
    c["carry"] = np.full((128, 1), float(carry), np.float32)
    q = np.arange(128)[:, None]
    kk = np.arange(384)[None, :]
    rel = kk - 128 - q
    base = np.where(np.abs(rel) <= 128, 0.0, NEG).astype(np.float32)
    noprev = base.copy()
    noprev[:, 0:128] = NEG
    nonext = base.copy()
    nonext[:, 256:384] = NEG
    am = np.stack([base, base if carry else noprev, base if carry else nonext, noprev, nonext], axis=1)
    c["amask"] = np.ascontiguousarray(am.astype(np.float32))
    a = np.arange(128)[:, None]
    b = np.arange(128)[None, :]
    NB_ = -1e30
    tri = np.stack([(a <= b).astype(np.float32), (a >= b).astype(np.float32),
                    np.where(a > b, 0.0, NB_), np.where(a < b, 0.0, NB_),
                    np.where(a <= b, 0.0, NB_), np.where(a >= b, 0.0, NB_)], axis=1)
    c["tri"] = np.ascontiguousarray(tri.astype(np.float32))
    return c


T_CORE, SEG_LEN, BLK1_LEN = 16384, 4096, 1024
_VEC = ("a_log", "dt_bias", "attn_sink", "dn_norm_w", "g_pre_mix", "g_post_mix", "g_pre_ffn", "g_post_ffn")
_NC_CACHE = {}


def kernel(x_prompt, x_sample, w_in, conv_w, a_log, dt_bias, attn_sink, dn_norm_w, w_attn_o, w_dn_o, w_out,
           w_gate, w_up, w_down, g_pre_mix, g_post_mix, g_pre_ffn, g_post_ffn):
    T, SEG, BLK1 = T_CORE, SEG_LEN, BLK1_LEN
    x_prompt = np.asarray(x_prompt, dtype=np.float32)
    x_sample = np.asarray(x_sample, dtype=np.float32)
    W = dict(w_in=w_in, conv_w=conv_w, a_log=a_log, dt_bias=dt_bias, attn_sink=attn_sink, dn_norm_w=dn_norm_w, w_attn_o=w_attn_o,
             w_dn_o=w_dn_o, w_out=w_out, w_gate=w_gate, w_up=w_up, w_down=w_down, g_pre_mix=g_pre_mix, g_post_mix=g_post_mix,
             g_pre_ffn=g_pre_ffn, g_post_ffn=g_post_ffn)
    Wc = {}
    for k, v in W.items():
        v = np.asarray(v, dtype=np.float32)[0]
        Wc[k] = np.ascontiguousarray(v.reshape(-1) if k in _VEC else v)
    if "nc" not in _NC_CACHE:
        _NC_CACHE["nc"] = build(T, SEG, BLK1, debug=False)
    nc = _NC_CACHE["nc"]
    consts = {1: host_consts(T, SEG, BLK1, 1), 0: host_consts(T, SEG, BLK1, 0)}
    in_maps = []
    for core in range(8):
        if core < 2:
            xs, carry = x_sample[core], 1
        else:
            pc = core - 2 if core < 6 else core - 4
            xs, carry = x_prompt[4 * pc:4 * pc + 4].reshape(T, D_MODEL), 0
        m = {"x": np.ascontiguousarray(xs)}
        m.update(Wc)
        m.update(consts[carry])
        in_maps.append(m)
    res = run_bass_kernel_spmd(nc, in_maps, core_ids=list(range(8)))
    ys = [np.asarray(res.results[c]["y"], dtype=np.float32) for c in range(8)]
    y_sample = np.stack([ys[0], ys[1]], axis=0)
    y_prompt = np.concatenate([ys[c].reshape(4, SEG, D_MODEL) for c in range(2, 6)], axis=0)
    return (y_prompt, y_sample)
```

```python
import contextlib
import os
import numpy as np
import concourse.bass as bass
import concourse.mybir as mybir
from concourse.bass_utils import run_bass_kernel_spmd

F32 = mybir.dt.float32
BF16 = mybir.dt.bfloat16
F32R = mybir.dt.float32r
ALU = mybir.AluOpType
AF = mybir.ActivationFunctionType
AX = mybir.AxisListType

PE, ACT, DVE, POOL, SP = "tensor", "scalar", "vector", "gpsimd", "sync"
ENGS = (PE, ACT, DVE, POOL, SP)

D_MODEL = 1024
HD = 128
NHQ = 8
NHKV = 2
DNH = 8
D_FF = 2816
NFF = D_FF // 128
IN_COLS = 7712
EPS = 1e-6
NEG = -30000.0
C_QA, C_KA, C_VA, C_DN, C_Z, C_AB, C_GA = 0, 1024, 1280, 1536, 4608, 5632, 5664


class Tok:
    __slots__ = ("name", "w", "rs")

    def __init__(self, name=""):
        self.name = name
        self.w = None
        self.rs = []


class Op:
    __slots__ = ("eng", "fn", "waits", "sig", "dma_key", "val", "is_dma", "ndma", "seq")


class Prog:
    def __init__(self, nc):
        self.nc = nc
        self.by_eng = {e: [] for e in ENGS}
        self.dma_cnt = {}
        self.dma_last = {}
        self.nops = 0

    def add(self, eng, fn, reads=(), writes=(), dma_key=None, ndma=1):
        op = Op()
        op.eng = eng
        op.fn = fn
        op.sig = False
        op.is_dma = dma_key is not None
        op.dma_key = dma_key
        op.ndma = ndma
        op.val = None
        op.seq = self.nops
        deps = {}
        for t in reads:
            if t.w is not None:
                deps[id(t.w)] = (t.w, True)
        for t in writes:
            if t.w is not None and id(t.w) not in deps:
                deps[id(t.w)] = (t.w, False)
            for r in t.rs:
                if id(r) not in deps:
                    deps[id(r)] = (r, False)
        if op.is_dma:
            prev = self.dma_last.get(dma_key)
            if prev is not None:
                deps[id(prev)] = (prev, True)
        best = {}
        for p, raw in deps.values():
            if p.is_dma:
                key = ("d", p.dma_key)
            elif op.is_dma or p.eng != op.eng or raw or op.eng != PE:
                key = ("e", p.eng)
            else:
                continue
            q = best.get(key)
            if q is None or p.seq > q.seq:
                best[key] = p
        waits = []
        for p in best.values():
            if not p.is_dma:
                p.sig = True
            waits.append(p)
        op.waits = waits
        for t in reads:
            t.rs.append(op)
        for t in writes:
            t.w = op
            t.rs = []
        if op.is_dma:
            self.dma_cnt[dma_key] = self.dma_cnt.get(dma_key, 0) + 16 * ndma
            op.val = self.dma_cnt[dma_key]
            self.dma_last[dma_key] = op
        self.by_eng[eng].append(op)
        self.nops += 1
        return op

    def barrier(self):
        lasts = []
        for e in ENGS:
            for op in reversed(self.by_eng[e]):
                if not op.is_dma and op.fn is not None:
                    op.sig = True
                    lasts.append(op)
                    break
        lasts += list(self.dma_last.values())
        for e in ENGS:
            op = Op()
            op.eng = e
            op.fn = None
            op.sig = False
            op.is_dma = False
            op.dma_key = None
            op.ndma = 0
            op.val = None
            op.seq = self.nops
            self.nops += 1
            op.waits = list(lasts)
            self.by_eng[e].append(op)

    def emit(self, final_waits=()):
        nc = self.nc
        for e in ENGS:
            c = 0
            for op in self.by_eng[e]:
                if not op.is_dma and op.sig:
                    c += 1
                    op.val = c
        keys = sorted(self.dma_cnt.keys())
        with contextlib.ExitStack() as st:
            esem = {e: st.enter_context(nc.semaphore("s_" + e)) for e in ENGS}
            dsem = {k: st.enter_context(nc.semaphore("d_%d" % i)) for i, k in enumerate(keys)}
            block = st.enter_context(nc.Block())

            def semof(p):
                return dsem[p.dma_key] if p.is_dma else esem[p.eng]

            def run(e, engobj, extra=None):
                waited = {}
                for op in self.by_eng[e]:
                    for p in op.waits:
                        s = semof(p)
                        k = id(s)
                        if waited.get(k, 0) >= p.val:
                            continue
                        waited[k] = p.val
                        engobj.wait_ge(s, p.val)
                    if op.fn is None:
                        continue
                    r = op.fn(engobj)
                    if op.is_dma:
                        rs = r if isinstance(r, (list, tuple)) else [r]
                        assert len(rs) == op.ndma, (len(rs), op.ndma)
                        for ins in rs:
                            ins.then_inc(dsem[op.dma_key], 16)
                    elif op.sig:
                        ins = r[-1] if isinstance(r, (list, tuple)) else r
                        ins.then_inc(esem[e], 1)
                if extra:
                    for p in extra:
                        s = semof(p)
                        if waited.get(id(s), 0) >= p.val:
                            continue
                        waited[id(s)] = p.val
                        engobj.wait_ge(s, p.val)

            @block.tensor
            def _(eng):
                run(PE, eng)

            @block.scalar
            def _(eng):
                run(ACT, eng)

            @block.vector
            def _(eng):
                run(DVE, eng)

            @block.gpsimd
            def _(eng):
                run(POOL, eng)

            @block.sync
            def _(eng):
                run(SP, eng, extra=list(final_waits))


class Ring:
    def __init__(self, bufs):
        self.bufs = bufs
        self.i = 0

    def next(self):
        b = self.bufs[self.i % len(self.bufs)]
        self.i += 1
        return b


def in_blocks():
    bl = []
    for h in range(NHQ):
        bl.append(("qa%d" % h, C_QA + h * 128, 128))
    for g in range(NHKV):
        bl.append(("ka%d" % g, C_KA + g * 128, 128))
    bl.append(("va", C_VA, 256))
    for c in range(24):
        bl.append(("dn%d" % c, C_DN + c * 128, 128))
    bl.append(("z0", C_Z, 512))
    bl.append(("z1", C_Z + 512, 512))
    bl.append(("ab", C_AB, 32))
    for c in range(16):
        bl.append(("g%d" % c, C_GA + c * 128, 128))
    return bl


class Ctx:
    pass


def build(T, SEG, BLK1, debug=False, phases=(0, 1, 2, 3, 4)):
    assert T % SEG == 0 and SEG % BLK1 == 0 and BLK1 % 512 == 0
    NT = T // 128
    NB1 = T // BLK1
    nc = bass.Bass("TRN2", target_bir_lowering=False)
    K = Ctx()
    K.nc, K.T, K.SEG, K.BLK1, K.NT, K.NB1 = nc, T, SEG, BLK1, NT, NB1
    K.debug = debug

    def din(name, shape, dt=F32):
        return nc.dram_tensor(name, list(shape), dt, kind="ExternalInput").ap()

    def dscr(name, shape, dt):
        kind = "ExternalOutput" if debug else "Internal"
        return nc.dram_tensor(name, list(shape), dt, kind=kind).ap()

    I = {}
    I["x"] = din("x", [T, D_MODEL])
    I["w_in"] = din("w_in", [D_MODEL, IN_COLS])
    I["conv_w"] = din("conv_w", [5, 3072])
    I["a_log"] = din("a_log", [16])
    I["dt_bias"] = din("dt_bias", [16])
    I["attn_sink"] = din("attn_sink", [8])
    I["dn_norm_w"] = din("dn_norm_w", [128])
    I["w_attn_o"] = din("w_attn_o", [1024, 1024])
    I["w_dn_o"] = din("w_dn_o", [1024, 1024])
    I["w_out"] = din("w_out", [1024, 1024])
    I["w_gate"] = din("w_gate", [1024, D_FF])
    I["w_up"] = din("w_up", [1024, D_FF])
    I["w_down"] = din("w_down", [D_FF, 1024])
    for g in ("g_pre_mix", "g_post_mix", "g_pre_ffn", "g_post_ffn"):
        I[g] = din(g, [1024])
    I["ident"] = din("ident", [128, 128])
    I["rotm"] = din("rotm", [128, 128])
    I["cosT"] = din("cosT", [32, T])
    I["sinT"] = din("sinT", [32, T])
    I["hflags"] = din("hflags", [4, NB1])
    I["carry"] = din("carry", [128, 1])
    I["amask"] = din("amask", [128, 5, 384])
    I["tri"] = din("tri", [128, 6, 128])
    K.I = I
    y = nc.dram_tensor("y", [T, D_MODEL], F32, kind="ExternalOutput").ap()
    K.y = y

    S = {}
    S["Wb_in"] = nc.dram_tensor("Wb_in", [128, 8 * IN_COLS], BF16, kind="Internal").ap()
    for nm in ("Wb_ao", "Wb_do", "Wb_out"):
        S[nm] = nc.dram_tensor(nm, [128, 8 * 1024], BF16, kind="Internal").ap()
    for nm in ("Wb_g", "Wb_u"):
        S[nm] = nc.dram_tensor(nm, [128, 8 * D_FF], BF16, kind="Internal").ap()
    S["Wb_d"] = nc.dram_tensor("Wb_d", [128, NFF * 1024], BF16, kind="Internal").ap()
    S["QA"] = dscr("QA", [8, 128, T], BF16)
    S["KA"] = dscr("KA", [2, 128, T], BF16)
    S["VA"] = dscr("VA", [T, 256], BF16)
    S["DQ"] = dscr("DQ", [8, 128, T], BF16)
    S["DK"] = dscr("DK", [8, 128, T], BF16)
    S["DKt"] = dscr("DKt", [T, 1024], BF16)
    S["DV"] = dscr("DV", [T, 1024], BF16)
    S["ZS"] = dscr("ZS", [T, 1024], F32)
    S["AB"] = dscr("AB", [T, 32], F32)
    S["GT"] = dscr("GT", [16, 128, T], F32)
    S["AT"] = dscr("AT", [8, 128, T], BF16)
    S["OF"] = dscr("OF", [T, 1024], F32)
    S["OB"] = dscr("OB", [T, 1024], F32)
    K.S = S
    K.tS = {k: Tok(k) for k in S}

    P = Prog(nc)
    K.P = P
    finals = []
    with contextlib.ExitStack() as gst:
        K.sb = lambda n, s, d: gst.enter_context(nc.sbuf_tensor(n, list(s), d))
        C = Ctx()
        K.C = C
        C.ident = K.sb("c_ident", [128, 128], F32)
        C.identb = K.sb("c_identb", [128, 128], BF16)
        C.onesb = K.sb("c_onesb", [128, 128], BF16)
        C.onesf = K.sb("c_onesf", [128, 128], F32)
        C.eps = K.sb("c_eps", [128, 1], F32)
        C.carry = K.sb("c_carry", [128, 1], F32)
        C.t_const = Tok("const")
        P.add(SP, lambda e: e.dma_start(out=C.ident[:], in_=I["ident"]), writes=[C.t_const], dma_key="c0")
        P.add(SP, lambda e: e.dma_start(out=C.carry[:], in_=I["carry"]), writes=[C.t_const], dma_key="c1")
        P.add(DVE, lambda e: e.tensor_copy(out=C.identb[:], in_=C.ident[:]), reads=[C.t_const], writes=[C.t_const])
        P.add(DVE, lambda e: e.memset(C.onesb[:], 1.0), writes=[C.t_const])
        P.add(DVE, lambda e: e.memset(C.onesf[:], 1.0), writes=[C.t_const])
        P.add(DVE, lambda e: e.memset(C.eps[:], EPS), writes=[C.t_const])
        if 0 in phases:
            phase0(K)
        if 1 in phases:
            phase1(K)
            P.barrier()
        if 2 in phases:
            phase2(K)
            P.barrier()
        if 3 in phases:
            phase3(K)
            P.barrier()
        if 4 in phases:
            finals += phase4(K)
        finals += getattr(K, "finals", [])
        P.emit(final_waits=finals)
    return nc


def phase0(K):
    P, I, S = K.P, K.I, K.S
    K.t_w = {}
    off = 0
    K.in_off = {}
    n = 0
    for (nm, c0, ncol) in in_blocks():
        K.in_off[nm] = (off, ncol)
        dst = S["Wb_in"][:, off:off + 8 * ncol].rearrange("p (kc c) -> p kc c", kc=8)
        src = I["w_in"][:, c0:c0 + ncol].rearrange("(kc p) c -> p kc c", p=128)
        t = Tok("w_" + nm)
        K.t_w["in_" + nm] = t
        P.add(POOL, lambda e, dst=dst, src=src: e.dma_start(out=dst, in_=src), writes=[t], dma_key="wc%d" % (n % 4))
        n += 1
        off += 8 * ncol
    for (sn, wn) in (("Wb_ao", "w_attn_o"), ("Wb_do", "w_dn_o"), ("Wb_out", "w_out")):
        for c in range(8):
            dst = S[sn][:, c * 1024:(c + 1) * 1024].rearrange("p (kc c) -> p kc c", kc=8)
            src = I[wn][:, c * 128:(c + 1) * 128].rearrange("(kc p) c -> p kc c", p=128)
            t = Tok()
            K.t_w["%s_%d" % (sn, c)] = t
            P.add(POOL, lambda e, dst=dst, src=src: e.dma_start(out=dst, in_=src), writes=[t], dma_key="wc%d" % (n % 4))
            n += 1
    for (sn, wn) in (("Wb_g", "w_gate"), ("Wb_u", "w_up")):
        for f in range(NFF):
            dst = S[sn][:, f * 1024:(f + 1) * 1024].rearrange("p (kc c) -> p kc c", kc=8)
            src = I[wn][:, f * 128:(f + 1) * 128].rearrange("(kc p) c -> p kc c", p=128)
            t = Tok()
            K.t_w["%s_%d" % (sn, f)] = t
            P.add(POOL, lambda e, dst=dst, src=src: e.dma_start(out=dst, in_=src), writes=[t], dma_key="wc%d" % (n % 4))
            n += 1
    for c in range(8):
        dst = S["Wb_d"][:, c * NFF * 128:(c + 1) * NFF * 128].rearrange("p (f c) -> p f c", f=NFF)
        src = I["w_down"][:, c * 128:(c + 1) * 128].rearrange("(f p) c -> p f c", p=128)
        t = Tok()
        K.t_w["Wb_d_%d" % c] = t
        P.add(POOL, lambda e, dst=dst, src=src: e.dma_start(out=dst, in_=src), writes=[t], dma_key="wc%d" % (n % 4))
        n += 1
    K.finals = [t.w for t in phase_barrier(K, ["wc%d" % i for i in range(4)])]


def phase1(K):
    nc, P, I, S, C = K.nc, K.P, K.I, K.S, K.C
    T, BLK1, NB1 = K.T, K.BLK1, K.NB1
    NTB = BLK1 // 128
    NMT = BLK1 // 512
    HW = BLK1 + 4
    with contextlib.ExitStack() as st:
        sb = lambda n, s, d: st.enter_context(nc.sbuf_tensor("p1_" + n, list(s), d))
        ps = lambda n, s, d: st.enter_context(nc.psum_tensor("p1_" + n, list(s), d))

        def mk(n, s, d, k=1, f=sb):
            return Ring([(f("%s%d" % (n, i), s, d), Tok("%s%d" % (n, i))) for i in range(k)])

        gexp = sb("gexp", [128, 8, 128], F32)
        gT = sb("gT", [128, 8], F32)
        cw = sb("cw", [128, 120], F32)
        rotm = sb("rotm", [128, 128], F32)
        hfl = sb("hfl", [4, NB1], F32)
        alog = sb("alog", [128, 16], F32)
        dtb = sb("dtb", [128, 16], F32)
        nA = sb("nA", [128, 16], F32)
        t_c1 = Tok("p1const")
        ldc = sb("ldc", [128, 128], F32)
        t_ldc = Tok("ldc")
        with contextlib.ExitStack() as st0:
            pc = st0.enter_context(nc.psum_tensor("p1_pc", [128, 128], F32))
            t_pc = Tok("pc")
            P.add(SP, lambda e: e.dma_start(out=ldc[0:8, :], in_=I["g_pre_mix"].rearrange("(kc p) -> kc p", p=128)), writes=[t_ldc], dma_key="c0")
            P.add(PE, lambda e: e.transpose(out=pc[:, 0:8], in_=ldc[0:8, :], identity=C.ident[0:8, 0:8]), reads=[t_ldc, C.t_const], writes=[t_pc])
            P.add(DVE, lambda e: e.tensor_copy(out=gT[:], in_=pc[:, 0:8]), reads=[t_pc], writes=[t_c1])
            P.add(SP, lambda e: e.dma_start(out=ldc[0:120, :], in_=I["conv_w"].rearrange("j (c p) -> (j c) p", p=128)), reads=[t_pc], writes=[t_ldc], dma_key="c0")
            P.add(PE, lambda e: e.transpose(out=pc[:, 0:120], in_=ldc[0:120, :], identity=C.ident[0:120, 0:120]), reads=[t_ldc, C.t_const], writes=[t_pc])
            P.add(DVE, lambda e: e.tensor_copy(out=cw[:], in_=pc[:, 0:120]), reads=[t_pc], writes=[t_c1])
        P.add(SP, lambda e: e.dma_start(out=rotm[:], in_=I["rotm"]), writes=[t_c1], dma_key="c0")
        P.add(SP, lambda e: e.dma_start(out=hfl[:], in_=I["hflags"]), writes=[t_c1], dma_key="c1")
        P.add(SP, lambda e: e.dma_start(out=alog[:], in_=I["a_log"].partition_broadcast(128)), writes=[t_c1], dma_key="c0")
        P.add(SP, lambda e: e.dma_start(out=dtb[:], in_=I["dt_bias"].partition_broadcast(128)), writes=[t_c1], dma_key="c1")
        P.add(DVE, lambda e: e.tensor_copy(out=gexp[:], in_=gT[:].unsqueeze(2).broadcast_to([128, 8, 128])), reads=[t_c1], writes=[t_c1])
        P.add(ACT, lambda e: e.activation(out=nA[:], in_=alog[:], func=AF.Exp), reads=[t_c1], writes=[t_c1])
        P.add(DVE, lambda e: e.tensor_scalar(out=nA[:], in0=nA[:], scalar1=-1.0, scalar2=None, op0=ALU.mult), reads=[t_c1], writes=[t_c1])

        xin = mk("xin", [128, 1024], F32, 3)
        xh = mk("xh", [4, 1024], F32, 1)
        junk = mk("junk", [128, 1024], BF16, 2)
        st1 = mk("st1", [128, 4], F32, 3)
        xn = mk("xn", [128, 1024], F32, 2)
        hT = sb("hT", [128, 8, HW], BF16)
        t_hT = Tok("hT")
        wblk = mk("wblk", [128, 8, 512], BF16, 3)
        pmain = mk("pm", [128, 512], F32, 4, ps)
        ptr = mk("ptr", [128, 4, 128], F32, 1, ps)
        paux = mk("paux", [128, 512], F32, 2, ps)
        ptb = mk("ptb", [128, 8, 128], BF16, 1, ps)
        qst = mk("qst", [128, BLK1], BF16, 2)
        pa32 = mk("pa32", [128, 512], F32, 12)
        rt1 = mk("rt1", [32, 512], F32, 2)
        rt2 = mk("rt2", [32, 512], F32, 2)
        cst = mk("cst", [32, BLK1], F32, 1)
        snt = mk("snt", [32, BLK1], F32, 1)
        pqc = mk("pqc", [128, HW], BF16, 7)
        acc = mk("acc", [128, BLK1], F32, 2)
        sqb = mk("sqb", [128, BLK1], BF16, 1)
        sd = mk("sd", [128, 512], F32, 2)
        nst = mk("nst", [128, BLK1], BF16, 2)
        vst = mk("vst", [128, NTB, 128], BF16, 1)
        kst = mk("kst", [128, NTB, 128], BF16, 1)
        gst = mk("gst", [128, 512], F32, 3)
        zst = mk("zst", [128, 512], F32, 3)
        vast = mk("vast", [128, 256], BF16, 2)
        abt = mk("abt", [128, 32], F32, 2)
        abo = mk("abo", [128, 32], F32, 2)

        blocks = in_blocks()
        nstore = [0]

        def store(out, in_, rd):
            k = "st%d" % (nstore[0] % 4)
            nstore[0] += 1
            return P.add(POOL, lambda e: e.dma_start(out=out, in_=in_), reads=rd, dma_key=k)

        for b in range(NB1):
            tb0 = b * BLK1
            def norm_rows(xt, t_x, npart, flag_ap):
                (jk, t_jk) = junk.next()
                (s1, t_s1) = st1.next()
                P.add(ACT, lambda e: e.activation(out=jk[0:npart, :], in_=xt[0:npart, :], func=AF.Square, accum_out=s1[0:npart, 0:1]),
                      reads=[t_x], writes=[t_jk, t_s1])
                P.add(ACT, lambda e: e.activation(out=s1[0:npart, 1:2], in_=s1[0:npart, 0:1], func=AF.Sqrt, bias=C.eps[0:npart, :], scale=1.0 / D_MODEL),
                      reads=[t_s1, C.t_const], writes=[t_s1])
                P.add(DVE, lambda e: e.reciprocal(out=s1[0:npart, 2:3], in_=s1[0:npart, 1:2]), reads=[t_s1], writes=[t_s1])
                if flag_ap is not None:
                    P.add(DVE, lambda e: e.tensor_tensor(out=s1[0:npart, 2:3], in0=s1[0:npart, 2:3], in1=flag_ap, op=ALU.mult),
                          reads=[t_s1, t_c1], writes=[t_s1])
                (xo, t_xo) = xn.next()
                P.add(DVE, lambda e: e.tensor_scalar(out=xo[0:npart, :], in0=xt[0:npart, :], scalar1=s1[0:npart, 2:3], scalar2=None, op0=ALU.mult),
                      reads=[t_x, t_s1], writes=[t_xo])
                return xo, t_xo

            for i in range(NTB):
                (xt, t_x) = xin.next()
                r0 = tb0 + i * 128
                P.add(SP, lambda e, xt=xt, r0=r0: e.dma_start(out=xt[:], in_=I["x"][r0:r0 + 128, :]), writes=[t_x], dma_key="xin%d" % ((xin.i - 1) % 3))
                xo, t_xo = norm_rows(xt, t_x, 128, None)
                for half in range(2):
                    (pt, t_pt) = ptr.next()
                    for j in range(4):
                        kc = half * 4 + j
                        P.add(PE, lambda e, pt=pt, xo=xo, kc=kc, j=j: e.transpose(out=pt[:, j, :], in_=xo[:, kc * 128:(kc + 1) * 128], identity=C.ident[:]),
                              reads=[t_xo, C.t_const], writes=[t_pt])
                    P.add(DVE, lambda e, pt=pt, half=half, i=i: e.tensor_tensor(out=hT[:, half * 4:(half + 1) * 4, i * 128:(i + 1) * 128], in0=pt[:],
                                                                               in1=gexp[:, half * 4:(half + 1) * 4, :], op=ALU.mult),
                          reads=[t_pt, t_c1], writes=[t_hT])
            (xt, t_x) = xh.next()
            lo = max(tb0 - 2, 0)
            hi = min(tb0 + BLK1, T - 2)
            P.add(SP, lambda e, xt=xt, lo=lo, hi=hi: [e.dma_start(out=xt[0:2, :], in_=I["x"][lo:lo + 2, :]), e.dma_start(out=xt[2:4, :], in_=I["x"][hi:hi + 2, :])],
                  writes=[t_x], dma_key="xh", ndma=2)
            xo, t_xo = norm_rows(xt, t_x, 4, hfl[:, b:b + 1])
            for half in range(2):
                (pt, t_pt) = ptr.next()
                for j in range(4):
                    kc = half * 4 + j
                    P.add(PE, lambda e, pt=pt, xo=xo, kc=kc, j=j: e.transpose(out=pt[:, j, 0:4], in_=xo[0:4, kc * 128:(kc + 1) * 128], identity=C.ident[0:4, 0:4]),
                          reads=[t_xo, C.t_const], writes=[t_pt])
                for side in range(2):
                    c0 = BLK1 + 2 * side
                    P.add(DVE, lambda e, pt=pt, half=half, side=side, c0=c0: e.tensor_tensor(out=hT[:, half * 4:(half + 1) * 4, c0:c0 + 2], in0=pt[:, :, side * 2:side * 2 + 2],
                                                                                          in1=gexp[:, half * 4:(half + 1) * 4, 0:2], op=ALU.mult),
                          reads=[t_pt, t_c1], writes=[t_hT])
            (cs, t_cs) = cst.next()
            (sn, t_sn) = snt.next()
            P.add(SP, lambda e, cs=cs, tb0=tb0: e.dma_start(out=cs[:], in_=I["cosT"][:, tb0:tb0 + BLK1]), writes=[t_cs], dma_key="cst")
            P.add(SP, lambda e, sn=sn, tb0=tb0: e.dma_start(out=sn[:], in_=I["sinT"][:, tb0:tb0 + BLK1]), writes=[t_sn], dma_key="snt")

            def wblock(nm, c0, ncol, tb0=tb0, cs=cs, t_cs=t_cs, sn=sn, t_sn=t_sn):
                (wb, t_wb) = wblk.next()
                off, _ = K.in_off[nm]
                src = S["Wb_in"][:, off:off + 8 * ncol].rearrange("p (kc c) -> p kc c", kc=8)
                P.add(SP, lambda e, wb=wb, src=src, ncol=ncol: e.dma_start(out=wb[:, :, 0:ncol], in_=src), reads=[K.t_w["in_" + nm]], writes=[t_wb],
                      dma_key="wblk%d" % ((wblk.i - 1) % 3))
                kind = nm[:2]
                if kind in ("qa", "ka", "dn") or nm[0] == "g":
                    p32s = []
                    if kind == "dn":
                        (pq, t_pq) = pqc.next()
                    for mt in range(NMT):
                        (pm, t_pm) = pmain.next()
                        for kc in range(8):
                            P.add(PE, lambda e, pm=pm, wb=wb, kc=kc, mt=mt: e.matmul(out=pm[:], lhsT=wb[:, kc, 0:128], rhs=hT[:, kc, mt * 512:(mt + 1) * 512],
                                                                                start=(kc == 0), stop=(kc == 7)),
                                  reads=[t_wb, t_hT], writes=[t_pm])
                        csl = slice(mt * 512, (mt + 1) * 512)
                        if kind in ("qa", "ka"):
                            (p32, t_p32) = pa32.next()
                            P.add(ACT, lambda e, pm=pm, p32=p32: e.activation(out=p32[:], in_=pm[:], func=AF.Copy), reads=[t_pm], writes=[t_p32])
                            p32s.append((p32, t_p32, csl))
                        elif kind == "dn":
                            P.add(ACT, lambda e, pm=pm, pq=pq, mt=mt: e.activation(out=pq[:, 2 + mt * 512:2 + (mt + 1) * 512], in_=pm[:], func=AF.Copy),
                                  reads=[t_pm], writes=[t_pq])
                        else:
                            cidx = int(nm[1:])
                            (gs, t_gs) = gst.next()
                            P.add(ACT, lambda e, pm=pm, gs=gs: e.activation(out=gs[:], in_=pm[:], func=AF.Sigmoid), reads=[t_pm], writes=[t_gs])
                            store(S["GT"][cidx, :, tb0 + mt * 512:tb0 + (mt + 1) * 512], gs[:], [t_gs])
                    if kind in ("qa", "ka"):
                        yield
                        (qs, t_qs) = qst.next()
                        for (p32, t_p32, csl) in p32s:
                            P.add(DVE, lambda e, p32=p32, qs=qs, csl=csl: e.tensor_copy(out=qs[:, csl], in_=p32[:]), reads=[t_p32], writes=[t_qs])
                            (px, t_px) = paux.next()
                            P.add(PE, lambda e, px=px, p32=p32: e.matmul(out=px[:], lhsT=rotm[:], rhs=p32[:], start=True, stop=True),
                                  reads=[t_p32, t_c1], writes=[t_px])
                            (r1, t_r1) = rt1.next()
                            (r2, t_r2) = rt2.next()
                            P.add(DVE, lambda e, r1=r1, p32=p32, csl=csl: e.tensor_tensor(out=r1[:], in0=p32[0:32, :], in1=cs[:, csl], op=ALU.mult),
                                  reads=[t_p32, t_cs], writes=[t_r1])
                            P.add(DVE, lambda e, r2=r2, px=px, csl=csl: e.tensor_tensor(out=r2[:], in0=px[0:32, :], in1=sn[:, csl], op=ALU.mult),
                                  reads=[t_px, t_sn], writes=[t_r2])
                            P.add(DVE, lambda e, r1=r1, r2=r2, qs=qs, csl=csl: e.tensor_tensor(out=qs[0:32, csl], in0=r1[:], in1=r2[:], op=ALU.add),
                                  reads=[t_r1, t_r2], writes=[t_qs])
                        hidx = int(nm[2:])
                        dst = S["QA"] if kind == "qa" else S["KA"]
                        store(dst[hidx, :, tb0:tb0 + BLK1], qs[:], [t_qs])
                    if kind == "dn":
                        cidx = int(nm[2:])
                        (px, t_px) = paux.next()
                        for kc in range(8):
                            P.add(PE, lambda e, px=px, wb=wb, kc=kc: e.matmul(out=px[:, 0:4], lhsT=wb[:, kc, 0:128], rhs=hT[:, kc, BLK1:BLK1 + 4],
                                                                        start=(kc == 0), stop=(kc == 7)),
                                  reads=[t_wb, t_hT], writes=[t_px])
                        P.add(ACT, lambda e, px=px, pq=pq: e.activation(out=pq[:, 0:2], in_=px[:, 0:2], func=AF.Copy), reads=[t_px], writes=[t_pq])
                        P.add(ACT, lambda e, px=px, pq=pq: e.activation(out=pq[:, BLK1 + 2:BLK1 + 4], in_=px[:, 2:4], func=AF.Copy), reads=[t_px], writes=[t_pq])
                        yield
                        (ac, t_ac) = acc.next()
                        P.add(DVE, lambda e, ac=ac, pq=pq, cidx=cidx: e.tensor_scalar(out=ac[:], in0=pq[:, 0:BLK1], scalar1=cw[:, cidx:cidx + 1], scalar2=None, op0=ALU.mult),
                              reads=[t_pq, t_c1], writes=[t_ac])
                        for j in range(1, 5):
                            P.add(DVE, lambda e, ac=ac, pq=pq, cidx=cidx, j=j: e.scalar_tensor_tensor(out=ac[:], in0=pq[:, j:j + BLK1], scalar=cw[:, j * 24 + cidx:j * 24 + cidx + 1], in1=ac[:],
                                                                                                   op0=ALU.mult, op1=ALU.add),
                                  reads=[t_pq, t_c1, t_ac], writes=[t_ac])
                        P.add(ACT, lambda e, ac=ac: e.activation(out=ac[:], in_=ac[:], func=AF.Silu), reads=[t_ac], writes=[t_ac])
                        if cidx >= 16:
                            hidx = cidx - 16
                            (vs, t_vs) = vst.next()
                            for i0 in range(0, NTB, 4):
                                (pt, t_pt) = ptr.next()
                                for j in range(4):
                                    P.add(PE, lambda e, pt=pt, ac=ac, i0=i0, j=j: e.transpose(out=pt[:, j, :], in_=ac[:, (i0 + j) * 128:(i0 + j + 1) * 128], identity=C.ident[:]),
                                          reads=[t_ac, C.t_const], writes=[t_pt])
                                P.add(ACT, lambda e, pt=pt, vs=vs, i0=i0: e.activation(out=vs[:, i0:i0 + 4, :], in_=pt[:], func=AF.Copy), reads=[t_pt], writes=[t_vs])
                            store(S["DV"][tb0:tb0 + BLK1, hidx * 128:(hidx + 1) * 128].rearrange("(i p) c -> p i c", p=128), vs[:], [t_vs])
                        else:
                            isq = cidx < 8
                            hidx = cidx % 8
                            (sq, t_sq) = sqb.next()
                            P.add(DVE, lambda e, sq=sq, ac=ac: e.tensor_tensor(out=sq[:], in0=ac[:], in1=ac[:], op=ALU.mult), reads=[t_ac], writes=[t_sq])
                            (ns, t_ns) = nst.next()
                            for mt in range(NMT):
                                csl = slice(mt * 512, (mt + 1) * 512)
                                (px, t_px) = paux.next()
                                P.add(PE, lambda e, px=px, sq=sq, csl=csl: e.matmul(out=px[:], lhsT=C.onesb[:], rhs=sq[:, csl], start=True, stop=True),
                                      reads=[t_sq, C.t_const], writes=[t_px])
                                (sdd, t_sd) = sd.next()
                                P.add(ACT, lambda e, px=px, sdd=sdd: e.activation(out=sdd[:], in_=px[:], func=AF.Sqrt, bias=C.eps[:], scale=1.0),
                                      reads=[t_px, C.t_const], writes=[t_sd])
                                P.add(DVE, lambda e, sdd=sdd: e.reciprocal(out=sdd[:], in_=sdd[:]), reads=[t_sd], writes=[t_sd])
                                qsc = float(HD ** -0.5) if isq else 1.0
                                P.add(DVE, lambda e, ns=ns, ac=ac, sdd=sdd, csl=csl, qsc=qsc: e.scalar_tensor_tensor(out=ns[:, csl], in0=ac[:, csl], scalar=qsc, in1=sdd[:],
                                                                                                               op0=ALU.mult, op1=ALU.mult),
                                      reads=[t_ac, t_sd], writes=[t_ns])
                            store((S["DQ"] if isq else S["DK"])[hidx, :, tb0:tb0 + BLK1], ns[:], [t_ns])
                            if not isq:
                                (ks, t_ks) = kst.next()
                                for i0 in range(0, NTB, 4):
                                    (pt, t_pt) = ptb.next()
                                    for j in range(4):
                                        P.add(PE, lambda e, pt=pt, ns=ns, i0=i0, j=j: e.transpose(out=pt[:, j, :], in_=ns[:, (i0 + j) * 128:(i0 + j + 1) * 128], identity=C.identb[:]),
                                              reads=[t_ns, C.t_const], writes=[t_pt])
                                    P.add(ACT, lambda e, pt=pt, ks=ks, i0=i0: e.activation(out=ks[:, i0:i0 + 4, :], in_=pt[:, 0:4, :], func=AF.Copy), reads=[t_pt], writes=[t_ks])
                                store(S["DKt"][tb0:tb0 + BLK1, hidx * 128:(hidx + 1) * 128].rearrange("(i p) c -> p i c", p=128), ks[:], [t_ks])
                else:
                    for i in range(NTB):
                        (pm, t_pm) = pmain.next()
                        r0 = tb0 + i * 128
                        for kc in range(8):
                            P.add(PE, lambda e, pm=pm, wb=wb, kc=kc, i=i, ncol=ncol: e.matmul(out=pm[:, 0:ncol], lhsT=hT[:, kc, i * 128:(i + 1) * 128], rhs=wb[:, kc, 0:ncol],
                                                                                        start=(kc == 0), stop=(kc == 7)),
                                  reads=[t_wb, t_hT], writes=[t_pm])
                        if nm == "va":
                            (vs, t_vs) = vast.next()
                            P.add(ACT, lambda e, pm=pm, vs=vs: e.activation(out=vs[:], in_=pm[:, 0:256], func=AF.Copy), reads=[t_pm], writes=[t_vs])
                            store(S["VA"][r0:r0 + 128, :], vs[:], [t_vs])
                        elif nm in ("z0", "z1"):
                            zc = 0 if nm == "z0" else 512
                            (zs, t_zs) = zst.next()
                            P.add(ACT, lambda e, pm=pm, zs=zs: e.activation(out=zs[:], in_=pm[:], func=AF.Silu), reads=[t_pm], writes=[t_zs])
                            store(S["ZS"][r0:r0 + 128, zc:zc + 512], zs[:], [t_zs])
                        else:
                            (at, t_at) = abt.next()
                            (ao, t_ao) = abo.next()
                            P.add(ACT, lambda e, pm=pm, at=at: e.activation(out=at[:, 0:32], in_=pm[:, 0:32], func=AF.Copy), reads=[t_pm], writes=[t_at])
                            P.add(ACT, lambda e, at=at, ao=ao: e.activation(out=ao[:, 16:32], in_=at[:, 16:32], func=AF.Sigmoid), reads=[t_at], writes=[t_ao])
                            P.add(DVE, lambda e, at=at: e.tensor_tensor(out=at[:, 0:16], in0=at[:, 0:16], in1=dtb[:], op=ALU.add), reads=[t_at, t_c1], writes=[t_at])
                            P.add(DVE, lambda e, at=at: e.tensor_scalar(out=at[:, 16:32], in0=at[:, 0:16], scalar1=-1.0, scalar2=None, op0=ALU.mult), reads=[t_at], writes=[t_at])
                            P.add(DVE, lambda e, at=at: e.tensor_tensor(out=at[:, 16:32], in0=at[:, 16:32], in1=at[:, 0:16], op=ALU.max), reads=[t_at], writes=[t_at])
                            P.add(ACT, lambda e, at=at: e.activation(out=at[:, 16:32], in_=at[:, 16:32], func=AF.Exp, scale=-1.0), reads=[t_at], writes=[t_at])
                            P.add(ACT, lambda e, at=at: e.activation(out=at[:, 16:32], in_=at[:, 16:32], func=AF.Ln, bias=1.0, scale=1.0), reads=[t_at], writes=[t_at])
                            P.add(DVE, lambda e, at=at: e.scalar_tensor_tensor(out=at[:, 0:16], in0=at[:, 0:16], scalar=0.0, in1=at[:, 16:32], op0=ALU.max, op1=ALU.add), reads=[t_at], writes=[t_at])
                            P.add(DVE, lambda e, at=at, ao=ao: e.tensor_tensor(out=ao[:, 0:16], in0=at[:, 0:16], in1=nA[:], op=ALU.mult), reads=[t_at, t_c1], writes=[t_ao])
                            store(S["AB"][r0:r0 + 128, :], ao[:], [t_ao])
            pend = []

            def exhaust(gcur):
                for _ in gcur:
                    pass
            dn_b = [bb for bb in blocks if bb[0].startswith("dn")]
            ot_b = [bb for bb in blocks if not bb[0].startswith("dn")]
            order = []
            while dn_b or ot_b:
                if ot_b:
                    order.append(ot_b.pop(0))
                if dn_b:
                    order.append(dn_b.pop(0))
            for (nm, c0, ncol) in order:
                gcur = wblock(nm, c0, ncol)
                try:
                    next(gcur)
                    pend.append(gcur)
                except StopIteration:
                    pass
                while len(pend) > 5:
                    exhaust(pend.pop(0))
            for gcur in pend:
                exhaust(gcur)
        K.bar1 = phase_barrier(K, ["st%d" % i for i in range(4)])
        K.finals = [t.w for t in K.bar1]


def phase_barrier(K, keys):
    toks = []
    for k in keys:
        op = K.P.dma_last.get(k)
        if op is not None:
            t = Tok("bar_" + k)
            t.w = op
            toks.append(t)
    return toks


def phase2(K):
    nc, P, I, S, C = K.nc, K.P, K.I, K.S, K.C
    T, SEG = K.T, K.SEG
    NB = T // 128
    BPS = SEG // 128
    bar = list(getattr(K, "bar1", []))
    scale = float(HD ** -0.5)
    with contextlib.ExitStack() as st:
        sb = lambda n, s, d: st.enter_context(nc.sbuf_tensor("p2_" + n, list(s), d))
        ps = lambda n, s, d: st.enter_context(nc.psum_tensor("p2_" + n, list(s), d))

        def mk(n, s, d, k=1, f=sb):
            return Ring([(f("%s%d" % (n, i), s, d), Tok("%s%d" % (n, i))) for i in range(k)])

        amask = sb("amask", [128, 5, 384], F32)
        sink = sb("sink", [128, 8], F32)
        kzero = sb("kzero", [128, 2, 128], BF16)
        vzero = sb("vzero", [128, 256], BF16)
        t_c2 = Tok("p2const")
        P.add(SP, lambda e: e.dma_start(out=amask[:], in_=I["amask"]), writes=[t_c2], dma_key="c0")
        P.add(SP, lambda e: e.dma_start(out=sink[:], in_=I["attn_sink"].partition_broadcast(128)), writes=[t_c2], dma_key="c1")
        P.add(DVE, lambda e: e.memset(kzero[:], 0.0), writes=[t_c2])
        P.add(DVE, lambda e: e.memset(vzero[:], 0.0), writes=[t_c2])
        kslots = [(sb("ks%d" % i, [128, 2, 128], BF16), Tok()) for i in range(4)]
        vslots = [(sb("vs%d" % i, [128, 256], BF16), Tok()) for i in range(4)]
        qblk = mk("qb", [128, 8, 128], BF16, 2)
        pss = mk("pss", [128, 512], F32, 2, ps)
        ptp = mk("ptp", [128, 8, 128], BF16, 2, ps)
        pop = mk("pop", [128, 4, 128], F32, 2, ps)
        smr = mk("sm", [128, 384], F32, 4)
        pur = mk("pu", [128, 384], F32, 2)
        pnr = mk("pn", [128, 384], BF16, 4)
        ptsr = mk("pts", [128, 3, 128], BF16, 4)
        str_ = mk("stt", [128, 8], F32, 8)
        aor = mk("ao", [128, 8, 128], BF16, 2)
        nst = [0]

        def load_kv(j):
            (ks, t_ks) = kslots[j % 4]
            (vs, t_vs) = vslots[j % 4]
            P.add(SP, lambda e, ks=ks, j=j: e.dma_start(out=ks[:], in_=S["KA"][:, :, j * 128:(j + 1) * 128].rearrange("g p t -> p g t")), reads=bar, writes=[t_ks],
                  dma_key="p2k%d" % (j % 4))
            P.add(SP, lambda e, vs=vs, j=j: e.dma_start(out=vs[:], in_=S["VA"][j * 128:(j + 1) * 128, :]), reads=bar, writes=[t_vs], dma_key="p2v%d" % (j % 4))

        def unit(qb, h, qt, t_qt, kind, slots, ao, t_ao, pobox):
            g = h // 4
            (pS, t_pS) = pss.next()
            for jj in range(3):
                (ks, t_ks) = slots[jj][0]
                P.add(PE, lambda e, ks=ks, jj=jj: e.matmul(out=pS[:, jj * 128:(jj + 1) * 128], lhsT=qt[:, h, :], rhs=ks[:, g, :], start=True, stop=True),
                      reads=[t_qt, t_ks], writes=[t_pS])
            (sm, t_sm) = smr.next()
            (stt, t_st) = str_.next()
            P.add(DVE, lambda e: e.scalar_tensor_tensor(out=sm[:], in0=pS[:, 0:384], scalar=scale, in1=amask[:, kind, :], op0=ALU.mult, op1=ALU.add),
                  reads=[t_pS, t_c2], writes=[t_sm])
            P.add(DVE, lambda e: e.tensor_reduce(out=stt[:, 0:1], in_=sm[:], axis=AX.X, op=ALU.max), reads=[t_sm], writes=[t_st])
            P.add(DVE, lambda e: e.tensor_scalar(out=stt[:, 1:2], in0=stt[:, 0:1], scalar1=sink[:, h:h + 1], scalar2=-1.0, op0=ALU.max, op1=ALU.mult),
                  reads=[t_st, t_c2], writes=[t_st])
            yield
            (pu, t_pu) = pur.next()
            P.add(ACT, lambda e: e.activation(out=pu[:], in_=sm[:], func=AF.Exp, bias=stt[:, 1:2], scale=1.0, accum_out=stt[:, 2:3]),
                  reads=[t_sm, t_st], writes=[t_pu, t_st])
            P.add(ACT, lambda e: e.activation(out=stt[:, 3:4], in_=sink[:, h:h + 1], func=AF.Exp, bias=stt[:, 1:2], scale=1.0),
                  reads=[t_st, t_c2], writes=[t_st])
            P.add(DVE, lambda e: e.tensor_tensor(out=stt[:, 4:5], in0=stt[:, 2:3], in1=stt[:, 3:4], op=ALU.add), reads=[t_st], writes=[t_st])
            P.add(DVE, lambda e: e.reciprocal(out=stt[:, 5:6], in_=stt[:, 4:5]), reads=[t_st], writes=[t_st])
            (pn, t_pn) = pnr.next()
            P.add(DVE, lambda e: e.tensor_scalar(out=pn[:], in0=pu[:], scalar1=stt[:, 5:6], scalar2=None, op0=ALU.mult), reads=[t_pu, t_st], writes=[t_pn])
            yield
            (pt, t_pt) = ptp.next()
            for jj in range(3):
                P.add(PE, lambda e, jj=jj: e.transpose(out=pt[:, jj, :], in_=pn[:, jj * 128:(jj + 1) * 128], identity=C.identb[:]),
                      reads=[t_pn, C.t_const], writes=[t_pt])
            (pts, t_pts) = ptsr.next()
            P.add(ACT, lambda e: e.activation(out=pts[:], in_=pt[:, 0:3, :], func=AF.Copy), reads=[t_pt], writes=[t_pts])
            yield
            if h % 4 == 0:
                pobox[0] = pop.next()
            (po, t_po) = pobox[0]
            for jj in range(3):
                (vs, t_vs) = slots[jj][1]
                P.add(PE, lambda e, vs=vs, jj=jj: e.matmul(out=po[:, h % 4, :], lhsT=vs[:, g * 128:(g + 1) * 128], rhs=pts[:, jj, :], start=(jj == 0), stop=(jj == 2)),
                      reads=[t_vs, t_pts], writes=[t_po])
            if h % 4 == 3:
                P.add(ACT, lambda e: e.activation(out=ao[:, g * 4:(g + 1) * 4, :], in_=po[:], func=AF.Copy), reads=[t_po], writes=[t_ao])
            if h == 7:
                k = "st%d" % (nst[0] % 4)
                nst[0] += 1
                P.add(POOL, lambda e: e.dma_start(out=S["AT"][:, :, qb * 128:(qb + 1) * 128].rearrange("h p t -> p h t"), in_=ao[:]), reads=[t_ao], dma_key=k)

        def unit_iter():
            load_kv(0)
            for qb in range(NB):
                if qb + 1 < NB:
                    load_kv(qb + 1)
                (qt, t_qt) = qblk.next()
                P.add(SP, lambda e, qt=qt, qb=qb: e.dma_start(out=qt[:], in_=S["QA"][:, :, qb * 128:(qb + 1) * 128].rearrange("h p t -> p h t")), reads=bar, writes=[t_qt],
                      dma_key="p2q%d" % ((qblk.i - 1) % 2))
                if qb == 0:
                    kind = 3
                elif qb == NB - 1:
                    kind = 4
                elif qb % BPS == 0:
                    kind = 1
                elif (qb + 1) % BPS == 0:
                    kind = 2
                else:
                    kind = 0
                slots = []
                for j in (qb - 1, qb, qb + 1):
                    if j < 0 or j >= NB:
                        slots.append(((kzero, t_c2), (vzero, t_c2)))
                    else:
                        slots.append((kslots[j % 4], vslots[j % 4]))
                (ao, t_ao) = aor.next()
                pobox = [None]
                for h in range(8):
                    yield unit(qb, h, qt, t_qt, kind, slots, ao, t_ao, pobox)

        W2 = 3
        it2 = unit_iter()
        active = []
        done_iter = False
        while True:
            while not done_iter and len(active) < W2:
                try:
                    active.append(next(it2))
                except StopIteration:
                    done_iter = True
            if not active:
                break
            for gcur in list(active):
                try:
                    next(gcur)
                except StopIteration:
                    active.remove(gcur)
        K.bar2 = phase_barrier(K, ["st%d" % i for i in range(4)])
        K.finals = [t.w for t in K.bar2]


def phase3(K):
    nc, P, I, S, C = K.nc, K.P, K.I, K.S, K.C
    T, SEG = K.T, K.SEG
    NT = T // 128
    BPS = SEG // 128
    bar = list(getattr(K, "bar1", []))
    with contextlib.ExitStack() as st:
        sb = lambda n, s, d: st.enter_context(nc.sbuf_tensor("p3_" + n, list(s), d))
        ps = lambda n, s, d: st.enter_context(nc.psum_tensor("p3_" + n, list(s), d))

        def mk(n, s, d, k=1):
            return Ring([(sb("%s%d" % (n, i), s, d), Tok("%s%d" % (n, i))) for i in range(k)])

        tri = sb("tri", [128, 6, 128], F32)
        t_c3 = Tok("p3const")
        P.add(SP, lambda e: e.dma_start(out=tri[:], in_=I["tri"]), writes=[t_c3], dma_key="c0")
        banks = [ps("bank%d" % i, [128, 4, 128], F32) for i in range(8)]
        btok = [Tok("bank%d" % i) for i in range(8)]
        BK = lambda b: (banks[b], btok[b])
        R_AB = Ring([(BK(0), BK(1)), (BK(5), BK(6))])
        bkX, bkXT = BK(2), BK(3)
        bkU = BK(4)
        bkR = BK(7)
        R_pG = Ring([(banks[7][:, 0, :], btok[7])])
        S32 = sb("S32", [128, 16, 128], F32)
        Sbf = sb("Sbf", [128, 16, 128], BF16)
        t_S = [Tok("S%d" % u) for u in range(4)]
        P.add(DVE, lambda e: e.memset(S32[:], 0.0), writes=t_S)
        P.add(DVE, lambda e: e.memset(Sbf[:], 0.0), writes=t_S)
        def dp(n, s, d, k=3):
            return [[(sb("%s_%d_%d" % (n, dd, p), s, d), Tok()) for p in range(k)] for dd in range(2)]
        B_ab = dp("ab", [128, 32], F32)
        B_gs = dp("gs", [128, 96], F32)
        B_kT = dp("kT", [128, 8, 128], BF16)
        B_qT = dp("qT", [128, 8, 128], BF16)
        B_V = dp("V", [128, 1024], BF16)
        R_Kt = mk("Kt", [128, 8, 128], BF16, 2)
        B_kd = dp("kd", [128, 8, 128], BF16)
        B_od = dp("od", [128, 1024], F32, 2)
        B_Tb = [[(sb("Tb_%d_%d" % (u, p), [128, 4, 128], BF16), Tok()) for p in range(2)] for u in range(4)]
        B_QK = [[(sb("QK_%d_%d" % (u, p), [128, 4, 128], BF16), Tok()) for p in range(2)] for u in range(4)]
        G4 = [128, 4, 128]
        R_rbA = mk("rbA", G4, F32, 3)
        R_rbB = mk("rbB", G4, F32, 3)
        R_t1 = mk("t1", G4, F32, 3)
        R_t2 = mk("t2", G4, F32, 3)
        R_AT = mk("AT", G4, F32, 4)
        R_A = mk("A", G4, F32, 4)
        R_X = mk("X", G4, F32, 6)
        R_XT = mk("XT", G4, F32, 6)
        R_Pt = mk("Pt", G4, F32, 4)
        R_Rt = mk("Rt", G4, F32, 2)
        R_Rp = mk("Rp", G4, BF16, 4)
        R_vn = mk("vn", G4, BF16, 4)
        R_o1 = mk("o1s", G4, F32, 4)
        R_tS = mk("tS", G4, F32, 2)
        nst = [0]

        def tile_of(n, d):
            return n if d == 0 else NT - 1 - n

        def prep(n):
            par = n % 3
            for d in range(2):
                it = tile_of(n, d)
                r0 = it * 128
                (ab, t_ab) = B_ab[d][par]; (gs, t_gs) = B_gs[d][par]; (kT, t_kT) = B_kT[d][par]; (qT, t_qT) = B_qT[d][par]
                (V, t_V) = B_V[d][par]; (Kt, t_Kt) = R_Kt.next(); (kd, t_kd) = B_kd[d][par]
                tag = "%d%d" % (d, par)
                P.add(SP, lambda e, ab=ab, r0=r0: e.dma_start(out=ab[:], in_=S["AB"][r0:r0 + 128, :]), reads=bar, writes=[t_ab], dma_key="p3ab" + tag)
                P.add(SP, lambda e, kT=kT, r0=r0: e.dma_start(out=kT[:], in_=S["DK"][:, :, r0:r0 + 128].rearrange("h p t -> p h t")), reads=bar, writes=[t_kT], dma_key="p3kT" + tag)
                P.add(SP, lambda e, qT=qT, r0=r0: e.dma_start(out=qT[:], in_=S["DQ"][:, :, r0:r0 + 128].rearrange("h p t -> p h t")), reads=bar, writes=[t_qT], dma_key="p3qT" + tag)
                P.add(SP, lambda e, V=V, r0=r0: e.dma_start(out=V[:], in_=S["DV"][r0:r0 + 128, :]), reads=bar, writes=[t_V], dma_key="p3V" + tag)
                P.add(SP, lambda e, Kt=Kt, r0=r0: e.dma_start(out=Kt[:].rearrange("p h c -> p (h c)"), in_=S["DKt"][r0:r0 + 128, :]), reads=bar, writes=[t_Kt], dma_key="p3Kt%d" % ((R_Kt.i - 1) % 2))
                g = ab[:, d * 8:(d + 1) * 8]
                beta = ab[:, 16 + d * 8:16 + (d + 1) * 8]
                (pG, t_pG) = R_pG.next()
                P.add(PE, lambda e, pG=pG, d=d, g=g: e.matmul(out=pG[:, 0:8], lhsT=tri[:, d, :], rhs=g, start=True, stop=True), reads=[t_c3, t_ab], writes=[t_pG])
                P.add(PE, lambda e, pG=pG, g=g: e.matmul(out=pG[:, 8:16], lhsT=C.onesf[:], rhs=g, start=True, stop=True), reads=[C.t_const, t_ab], writes=[t_pG])
                P.add(DVE, lambda e, pG=pG, gs=gs: e.tensor_copy(out=gs[:, 0:16], in_=pG[:, 0:16]), reads=[t_pG], writes=[t_gs])
                P.add(ACT, lambda e, gs=gs: e.activation(out=gs[:, 16:24], in_=gs[:, 0:8], func=AF.Exp), reads=[t_gs], writes=[t_gs])
                P.add(DVE, lambda e, gs=gs: e.tensor_scalar(out=gs[:, 24:32], in0=gs[:, 16:24], scalar1=-1.0, scalar2=None, op0=ALU.mult), reads=[t_gs], writes=[t_gs])
                P.add(DVE, lambda e, gs=gs: e.tensor_tensor(out=gs[:, 80:88], in0=gs[:, 8:16], in1=gs[:, 0:8], op=ALU.subtract), reads=[t_gs], writes=[t_gs])
                P.add(ACT, lambda e, gs=gs: e.activation(out=gs[:, 32:40], in_=gs[:, 80:88], func=AF.Exp), reads=[t_gs], writes=[t_gs])
                P.add(ACT, lambda e, gs=gs: e.activation(out=gs[:, 40:48], in_=gs[:, 8:16], func=AF.Exp), reads=[t_gs], writes=[t_gs])
                P.add(DVE, lambda e, gs=gs: e.tensor_scalar(out=gs[:, 48:56], in0=gs[:, 0:8], scalar1=-1.0, scalar2=None, op0=ALU.mult), reads=[t_gs], writes=[t_gs])
                P.add(DVE, lambda e, gs=gs, beta=beta: e.tensor_scalar(out=gs[:, 64:72], in0=beta, scalar1=1e-30, scalar2=None, op0=ALU.max), reads=[t_ab, t_gs], writes=[t_gs])
                P.add(ACT, lambda e, gs=gs: e.activation(out=gs[:, 56:64], in_=gs[:, 64:72], func=AF.Ln), reads=[t_gs], writes=[t_gs])
                P.add(DVE, lambda e, kd=kd, Kt=Kt, gs=gs: e.tensor_tensor(out=kd[:], in0=Kt[:], in1=gs[:, 32:40].unsqueeze(2).broadcast_to([128, 8, 128]), op=ALU.mult),
                      reads=[t_Kt, t_gs], writes=[t_kd])

        def bc_h(ap2):
            return ap2.unsqueeze(2).broadcast_to(G4)

        def bc_m(ap2):
            return ap2.unsqueeze(1).broadcast_to(G4)

        def tcomp(n, d, hg):
            par = n % 3
            par2 = n % 2
            u = d * 2 + hg
            h0 = hg * 4
            (ab, t_ab) = B_ab[d][par]; (gs, t_gs) = B_gs[d][par]; (kT, t_kT) = B_kT[d][par]; (qT, t_qT) = B_qT[d][par]
            g4 = ab[:, d * 8 + h0:d * 8 + h0 + 4]
            beta4 = ab[:, 16 + d * 8 + h0:16 + d * 8 + h0 + 4]
            (rbA, t_rbA) = R_rbA.next(); (rbB, t_rbB) = R_rbB.next()
            P.add(DVE, lambda e: e.tensor_tensor(out=rbA[:], in0=bc_m(tri[:, d, :]), in1=bc_h(g4), op=ALU.mult), reads=[t_c3, t_ab], writes=[t_rbA])
            P.add(DVE, lambda e: e.tensor_tensor(out=rbB[:], in0=bc_m(C.ident[:]), in1=bc_h(gs[:, 56 + h0:60 + h0]), op=ALU.mult), reads=[C.t_const, t_gs], writes=[t_rbB])
            P.add(DVE, lambda e: e.tensor_tensor(out=rbB[:], in0=rbB[:], in1=rbA[:], op=ALU.add), reads=[t_rbB, t_rbA], writes=[t_rbB])
            ((pA_, t_pA_), (pB_, t_pB_)) = R_AB.next()
            P.add(PE, lambda e: e.matmul(out=pA_[:].rearrange("p a c -> p (a c)"), lhsT=C.onesf[:], rhs=rbA[:].rearrange("p a c -> p (a c)"), start=True, stop=True),
                  reads=[C.t_const, t_rbA], writes=[t_pA_])
            P.add(PE, lambda e: e.matmul(out=pB_[:].rearrange("p a c -> p (a c)"), lhsT=C.onesf[:], rhs=rbB[:].rearrange("p a c -> p (a c)"), start=True, stop=True),
                  reads=[C.t_const, t_rbB], writes=[t_pB_])
            yield
            (t1, t_t1) = R_t1.next(); (t2, t_t2) = R_t2.next()
            ms, mi = (3, 4) if d == 0 else (2, 5)
            negG4 = gs[:, 48 + h0:52 + h0]
            P.add(DVE, lambda e: e.tensor_tensor(out=t1[:], in0=pB_[:], in1=bc_h(negG4), op=ALU.add), reads=[t_pB_, t_gs], writes=[t_t1])
            P.add(DVE, lambda e: e.tensor_tensor(out=t1[:], in0=t1[:], in1=bc_m(tri[:, ms, :]), op=ALU.add), reads=[t_t1, t_c3], writes=[t_t1])
            P.add(ACT, lambda e: e.activation(out=t1[:], in_=t1[:], func=AF.Exp), reads=[t_t1], writes=[t_t1])
            P.add(DVE, lambda e: e.tensor_tensor(out=t2[:], in0=pA_[:], in1=bc_h(negG4), op=ALU.add), reads=[t_pA_, t_gs], writes=[t_t2])
            P.add(DVE, lambda e: e.tensor_tensor(out=t2[:], in0=t2[:], in1=bc_m(tri[:, mi, :]), op=ALU.add), reads=[t_t2, t_c3], writes=[t_t2])
            P.add(ACT, lambda e: e.activation(out=t2[:], in_=t2[:], func=AF.Exp), reads=[t_t2], writes=[t_t2])
            for j in range(4):
                P.add(PE, lambda e, j=j: e.matmul(out=pA_[:, j, :], lhsT=kT[:, h0 + j, :], rhs=kT[:, h0 + j, :], start=True, stop=True), reads=[t_kT], writes=[t_pA_])
            for j in range(4):
                P.add(PE, lambda e, j=j: e.matmul(out=pB_[:, j, :], lhsT=kT[:, h0 + j, :], rhs=qT[:, h0 + j, :], start=True, stop=True), reads=[t_kT, t_qT], writes=[t_pB_])
            yield
            (AT, t_AT) = R_AT.next()
            (QK, t_QK) = B_QK[u][par2]
            P.add(DVE, lambda e: e.tensor_tensor(out=AT[:].bitcast(F32R), in0=pA_[:], in1=t1[:], op=ALU.mult), reads=[t_pA_, t_t1], writes=[t_AT])
            P.add(DVE, lambda e: e.tensor_tensor(out=QK[:], in0=pB_[:], in1=t2[:], op=ALU.mult), reads=[t_pB_, t_t2], writes=[t_QK])
            yield
            (pX, t_pX) = bkX; (pXT, t_pXT) = bkXT; (pU, t_pU) = bkU
            (A, t_A) = R_A.next()
            (Pt, t_Pt) = R_Pt.next()
            for j in range(4):
                P.add(PE, lambda e, j=j: e.transpose(out=pX[:, j, :], in_=AT[:, j, :], identity=C.ident[:]), reads=[t_AT, C.t_const], writes=[t_pX])
            P.add(ACT, lambda e: e.activation(out=A[:].bitcast(F32R), in_=pX[:], func=AF.Copy), reads=[t_pX], writes=[t_A])
            P.add(DVE, lambda e: e.tensor_tensor(out=Pt[:].bitcast(F32R), in0=bc_m(C.ident[:]), in1=AT[:], op=ALU.subtract), reads=[C.t_const, t_AT], writes=[t_Pt])
            yield
            X, t_X, XT, t_XT = A, t_A, AT, t_AT
            prev = None

            def p_update(Xp, t_Xp):
                for j in range(4):
                    P.add(PE, lambda e, j=j: e.matmul(out=pU[:, j, :], lhsT=Xp[:, j, :].bitcast(F32R), rhs=Pt[:, j, :].bitcast(F32R), start=True, stop=True), reads=[t_Xp, t_Pt], writes=[t_pU])
                P.add(DVE, lambda e: e.tensor_tensor(out=Pt[:].bitcast(F32R), in0=pU[:], in1=Pt[:], op=ALU.add), reads=[t_pU, t_Pt], writes=[t_Pt])

            for k in range(1, 7):
                (Xn, t_Xn) = R_X.next()
                for j in range(4):
                    P.add(PE, lambda e, j=j, X=X, XT=XT: e.matmul(out=pX[:, j, :], lhsT=XT[:, j, :].bitcast(F32R), rhs=X[:, j, :].bitcast(F32R), start=True, stop=True), reads=[t_X, t_XT], writes=[t_pX])
                P.add(ACT, lambda e, Xn=Xn: e.activation(out=Xn[:].bitcast(F32R), in_=pX[:], func=AF.Copy), reads=[t_pX], writes=[t_Xn])
                if k < 6:
                    (XTn, t_XTn) = R_XT.next()
                    for j in range(4):
                        P.add(PE, lambda e, j=j, X=X, XT=XT: e.matmul(out=pXT[:, j, :], lhsT=X[:, j, :].bitcast(F32R), rhs=XT[:, j, :].bitcast(F32R), start=True, stop=True), reads=[t_X, t_XT], writes=[t_pXT])
                    P.add(ACT, lambda e, XTn=XTn: e.activation(out=XTn[:].bitcast(F32R), in_=pXT[:], func=AF.Copy), reads=[t_pXT], writes=[t_XTn])
                else:
                    XTn, t_XTn = None, None
                if prev is not None:
                    p_update(*prev)
                yield
                prev = (Xn, t_Xn)
                X, t_X, XT, t_XT = Xn, t_Xn, XTn, t_XTn
            p_update(*prev)
            (Tb, t_Tb) = B_Tb[u][par2]
            P.add(DVE, lambda e: e.tensor_tensor(out=Tb[:], in0=Pt[:], in1=bc_h(beta4), op=ALU.mult), reads=[t_Pt, t_ab], writes=[t_Tb])

        def rec(n, d, hg):
            par = n % 3
            par2 = n % 2
            u = d * 2 + hg
            h0 = hg * 4
            us = slice(d * 8 + h0, d * 8 + h0 + 4)
            (gs, t_gs) = B_gs[d][par]; (kT, t_kT) = B_kT[d][par]; (qT, t_qT) = B_qT[d][par]
            (V, t_V) = B_V[d][par]; (kd, t_kd) = B_kd[d][par]; (od, t_od) = B_od[d][par2]
            (Tb, t_Tb) = B_Tb[u][par2]; (QK, t_QK) = B_QK[u][par2]
            tS = t_S[u]
            (p1, t_p1) = bkR
            for j in range(4):
                P.add(PE, lambda e, j=j: e.matmul(out=p1[:, j, :], lhsT=kT[:, h0 + j, :], rhs=Sbf[:, d * 8 + h0 + j, :], start=True, stop=True), reads=[t_kT, tS], writes=[t_p1])
            (Rt, t_Rt) = R_Rt.next(); (Rp, t_Rp) = R_Rp.next(); (o1s, t_o1s) = R_o1.next()
            V4 = V[:, h0 * 128:(h0 + 4) * 128].rearrange("p (a c) -> p a c", a=4)
            P.add(DVE, lambda e: e.tensor_tensor(out=Rt[:], in0=p1[:], in1=bc_h(gs[:, 24 + h0:28 + h0]), op=ALU.mult), reads=[t_p1, t_gs], writes=[t_Rt])
            P.add(DVE, lambda e: e.tensor_tensor(out=Rp[:], in0=Rt[:], in1=V4, op=ALU.add), reads=[t_Rt, t_V], writes=[t_Rp])
            for j in range(4):
                P.add(PE, lambda e, j=j: e.matmul(out=p1[:, j, :], lhsT=qT[:, h0 + j, :], rhs=Sbf[:, d * 8 + h0 + j, :], start=True, stop=True), reads=[t_qT, tS], writes=[t_p1])
            P.add(DVE, lambda e: e.tensor_tensor(out=o1s[:], in0=p1[:], in1=bc_h(gs[:, 16 + h0:20 + h0]), op=ALU.mult), reads=[t_p1, t_gs], writes=[t_o1s])
            yield
            (vn, t_vn) = R_vn.next()
            for j in range(4):
                P.add(PE, lambda e, j=j: e.matmul(out=p1[:, j, :], lhsT=Tb[:, j, :], rhs=Rp[:, j, :], start=True, stop=True), reads=[t_Tb, t_Rp], writes=[t_p1])
            P.add(DVE, lambda e: e.tensor_copy(out=vn[:], in_=p1[:]), reads=[t_p1], writes=[t_vn])
            yield
            for j in range(4):
                P.add(PE, lambda e, j=j: e.matmul(out=p1[:, j, :], lhsT=QK[:, j, :], rhs=vn[:, j, :], start=True, stop=True), reads=[t_QK, t_vn], writes=[t_p1])
            od4 = od[:, h0 * 128:(h0 + 4) * 128].rearrange("p (a c) -> p a c", a=4)
            P.add(DVE, lambda e: e.tensor_tensor(out=od4, in0=p1[:], in1=o1s[:], op=ALU.add), reads=[t_p1, t_o1s], writes=[t_od])
            (tSb, t_tSb) = R_tS.next()
            P.add(DVE, lambda e: e.tensor_tensor(out=tSb[:], in0=S32[:, us, :], in1=bc_h(gs[:, 40 + h0:44 + h0]), op=ALU.mult), reads=[tS, t_gs], writes=[t_tSb])
            for j in range(4):
                P.add(PE, lambda e, j=j: e.matmul(out=p1[:, j, :], lhsT=kd[:, h0 + j, :], rhs=vn[:, j, :], start=True, stop=True), reads=[t_kd, t_vn], writes=[t_p1])
            P.add(DVE, lambda e: e.tensor_tensor(out=S32[:, us, :], in0=p1[:], in1=tSb[:], op=ALU.add), reads=[t_p1, t_tSb, tS], writes=[tS])
            yield
            P.add(ACT, lambda e: e.activation(out=Sbf[:, us, :], in_=S32[:, us, :], func=AF.Copy), reads=[tS], writes=[tS])

        def run_windows(groups):
            pending = [list(g) for g, _ in groups]
            active = [[] for _ in groups]
            while any(pending) or any(active):
                for gi, (_, w) in enumerate(groups):
                    while len(active[gi]) < w and pending[gi]:
                        active[gi].append(pending[gi].pop(0))
                for gi in range(len(groups)):
                    for gcur in list(active[gi]):
                        try:
                            next(gcur)
                        except StopIteration:
                            active[gi].remove(gcur)

        prep(0)
        for n in range(-1, NT):
            gens = []
            if n + 2 < NT:
                prep(n + 2)
            if n + 1 < NT:
                gens += [tcomp(n + 1, d, hg) for hg in range(2) for d in range(2)]
            if n >= 0:
                for d in range(2):
                    it = tile_of(n, d)
                    seg_start = (it % BPS == 0) if d == 0 else (it % BPS == BPS - 1)
                    if n > 0 and seg_start:
                        toks = t_S[d * 2:(d + 1) * 2]
                        P.add(DVE, lambda e, d=d: e.tensor_scalar(out=S32[:, d * 8:(d + 1) * 8, :], in0=S32[:, d * 8:(d + 1) * 8, :], scalar1=C.carry[:, 0:1], scalar2=None, op0=ALU.mult),
                              reads=toks + [C.t_const], writes=toks)
                        P.add(ACT, lambda e, d=d: e.activation(out=Sbf[:, d * 8:(d + 1) * 8, :], in_=S32[:, d * 8:(d + 1) * 8, :], func=AF.Copy), reads=toks, writes=toks)
                recs = [rec(n, d, hg) for hg in range(2) for d in range(2)]
            else:
                recs = []
            run_windows([(recs, int(os.environ.get("W_REC", "2"))), (gens, int(os.environ.get("W_TC", "2")))])
            if n >= 0:
                par = n % 2
                for d in range(2):
                    it = tile_of(n, d)
                    (od, t_od) = B_od[d][par]
                    k = "st%d" % (nst[0] % 4)
                    nst[0] += 1
                    dst = S["OF"] if d == 0 else S["OB"]
                    P.add(POOL, lambda e, od=od, it=it, dst=dst: e.dma_start(out=dst[it * 128:(it + 1) * 128, :], in_=od[:]), reads=[t_od], dma_key=k)
        K.bar3 = phase_barrier(K, ["st%d" % i for i in range(4)])
        K.finals = [t.w for t in K.bar3]


def phase4(K):
    nc, P, I, S, C = K.nc, K.P, K.I, K.S, K.C
    T = K.T
    NM = T // 512
    bar = list(getattr(K, "bar1", [])) + list(getattr(K, "bar2", [])) + list(getattr(K, "bar3", []))
    finals = []
    with contextlib.ExitStack() as st:
        sb = lambda n, s, d: st.enter_context(nc.sbuf_tensor("p4_" + n, list(s), d))
        ps = lambda n, s, d: st.enter_context(nc.psum_tensor("p4_" + n, list(s), d))

        def mk(n, s, d, k=1, f=sb):
            return Ring([(f("%s%d" % (n, i), s, d), Tok("%s%d" % (n, i))) for i in range(k)])

        ptr = mk("ptr", [128, 4, 128], F32, 2, ps)
        ptb = mk("ptb", [128, 8, 128], BF16, 1, ps)
        pmm = mk("pmm", [128, 512], F32, 4, ps)
        pstat = mk("pstat", [128, 512], F32, 1, ps)
        gvec = sb("gvec", [128, 24], F32)
        nw = sb("nw", [128, 128], F32)
        ldc = sb("ldc", [24, 128], F32)
        t_c4 = Tok("p4const")
        t_ldc = Tok()
        for gi, gname in enumerate(("g_post_mix", "g_pre_ffn", "g_post_ffn")):
            P.add(SP, lambda e, gi=gi, gname=gname: e.dma_start(out=ldc[gi * 8:(gi + 1) * 8, :], in_=I[gname].rearrange("(kc p) -> kc p", p=128)), writes=[t_ldc], dma_key="c%d" % (gi % 2))
        (pt0, t_pt0) = ptr.next()
        P.add(PE, lambda e: e.transpose(out=pt0[:, 0, 0:24], in_=ldc[0:24, :], identity=C.ident[0:24, 0:24]), reads=[t_ldc, C.t_const], writes=[t_pt0])
        P.add(ACT, lambda e: e.activation(out=gvec[:], in_=pt0[:, 0, 0:24], func=AF.Copy), reads=[t_pt0], writes=[t_c4])
        P.add(SP, lambda e: e.dma_start(out=nw[:], in_=I["dn_norm_w"].partition_broadcast(128)), writes=[t_c4], dma_key="c0")

        ofr = mk("of", [128, 1024], F32, 1)
        obr = mk("ob", [128, 1024], F32, 1)
        zsr = mk("zs", [128, 1024], F32, 1)
        sqtr = mk("sqt", [128, 1024], F32, 1)
        dnbr = mk("dnb", [128, 1024], BF16, 1)
        st8 = mk("st8", [128, 24], F32, 2)
        xinr = mk("xin", [128, 1024], F32, 2)
        yor = mk("yo", [128, 1024], F32, 2)
        aTr = mk("aT", [128, 8, 512], BF16, 1)
        dnTr = mk("dnT", [128, 8, 512], BF16, 1)
        xTr = mk("xT", [128, 8, 512], F32, 1)
        mixTr = mk("mixT", [128, 8, 512], BF16, 1)
        moTr = mk("moT", [128, 8, 512], F32, 1)
        h2Tr = mk("h2T", [128, 8, 512], BF16, 1)
        actTr = mk("actT", [128, NFF, 512], BF16, 1)
        wsm = mk("wsm", [128, 8, 128], BF16, 8)
        wdr = mk("wd", [128, NFF, 128], BF16, 3)
        gar = mk("ga", [128, 512], F32, 2)
        gdr = mk("gd", [128, 512], F32, 2)
        tr1 = mk("t1", [128, 512], F32, 1)
        tr2 = mk("t2", [128, 512], F32, 1)
        sgr = mk("sg", [128, 512], F32, 2)
        sqmr = mk("sqm", [128, 512], BF16, 2)
        rsr = mk("rs", [128, 512], F32, 2)
        tmr = mk("tm", [128, 512], F32, 2)
        nst = [0]

        def wload(ring, scr, off, nkc, tname):
            (w, t_w) = ring.next()
            idx = (ring.i - 1) % len(ring.bufs)
            src = S[scr][:, off:off + nkc * 128].rearrange("p (kc c) -> p kc c", kc=nkc)
            P.add(SP, lambda e, w=w, src=src: e.dma_start(out=w[:], in_=src), reads=[K.t_w[tname]], writes=[t_w], dma_key="p4w%s%d" % ("s" if nkc == 8 else "d", idx))
            return w, t_w

        def stat_norm(srcT, t_src, scale):
            (pst, t_pst) = pstat.next()
            for c in range(8):
                (sqm, t_sqm) = sqmr.next()
                P.add(ACT, lambda e, sqm=sqm, c=c: e.activation(out=sqm[:], in_=srcT[:, c, :], func=AF.Square), reads=[t_src], writes=[t_sqm])
                P.add(PE, lambda e, pst=pst, sqm=sqm, c=c: e.matmul(out=pst[:], lhsT=C.onesb[:], rhs=sqm[:], start=(c == 0), stop=(c == 7)), reads=[t_sqm, C.t_const], writes=[t_pst])
            (rs, t_rs) = rsr.next()
            P.add(ACT, lambda e, rs=rs, pst=pst: e.activation(out=rs[:], in_=pst[:], func=AF.Sqrt, bias=C.eps[:], scale=scale), reads=[t_pst, C.t_const], writes=[t_rs])
            P.add(DVE, lambda e, rs=rs: e.reciprocal(out=rs[:], in_=rs[:]), reads=[t_rs], writes=[t_rs])
            return rs, t_rs

        for m in range(NM):
            t0 = m * 512
            (aT, t_aT) = aTr.next(); (dnT, t_dnT) = dnTr.next(); (xT, t_xT) = xTr.next(); (mixT, t_mixT) = mixTr.next()
            (moT, t_moT) = moTr.next(); (h2T, t_h2T) = h2Tr.next(); (actT, t_actT) = actTr.next()
            P.add(SP, lambda e, aT=aT, t0=t0: e.dma_start(out=aT[:], in_=S["AT"][:, :, t0:t0 + 512].rearrange("h p t -> p h t")), reads=bar, writes=[t_aT], dma_key="p4aT")
            for i in range(4):
                r0 = t0 + i * 128
                (of, t_of) = ofr.next(); (ob, t_ob) = obr.next(); (zs, t_zs) = zsr.next(); (xt, t_xt) = xinr.next()
                P.add(SP, lambda e, of=of, r0=r0: e.dma_start(out=of[:], in_=S["OF"][r0:r0 + 128, :]), reads=bar, writes=[t_of], dma_key="p4of")
                P.add(SP, lambda e, ob=ob, r0=r0: e.dma_start(out=ob[:], in_=S["OB"][r0:r0 + 128, :]), reads=bar, writes=[t_ob], dma_key="p4ob")
                P.add(SP, lambda e, zs=zs, r0=r0: e.dma_start(out=zs[:], in_=S["ZS"][r0:r0 + 128, :]), reads=bar, writes=[t_zs], dma_key="p4zs")
                P.add(SP, lambda e, xt=xt, r0=r0: e.dma_start(out=xt[:], in_=I["x"][r0:r0 + 128, :]), writes=[t_xt], dma_key="p4x%d" % ((xinr.i - 1) % 2))
                P.add(DVE, lambda e, of=of, ob=ob: e.tensor_tensor(out=of[:], in0=of[:], in1=ob[:], op=ALU.add), reads=[t_of, t_ob], writes=[t_of])
                (sqt, t_sqt) = sqtr.next()
                (s8, t_s8) = st8.next()
                P.add(ACT, lambda e, sqt=sqt, of=of: e.activation(out=sqt[:], in_=of[:], func=AF.Square), reads=[t_of], writes=[t_sqt])
                P.add(DVE, lambda e, s8=s8, sqt=sqt: e.tensor_reduce(out=s8[:, 0:8], in_=sqt[:].rearrange("p (h c) -> p h c", h=8), axis=AX.X, op=ALU.add), reads=[t_sqt], writes=[t_s8])
                P.add(ACT, lambda e, s8=s8: e.activation(out=s8[:, 8:16], in_=s8[:, 0:8], func=AF.Sqrt, bias=C.eps[:], scale=1.0 / 128.0), reads=[t_s8, C.t_const], writes=[t_s8])
                P.add(DVE, lambda e, s8=s8: e.reciprocal(out=s8[:, 16:24], in_=s8[:, 8:16]), reads=[t_s8], writes=[t_s8])
                o3 = of[:].rearrange("p (h c) -> p h c", h=8)
                P.add(DVE, lambda e, o3=o3, s8=s8: e.tensor_tensor(out=o3, in0=o3, in1=s8[:, 16:24].unsqueeze(2).broadcast_to([128, 8, 128]), op=ALU.mult), reads=[t_of, t_s8], writes=[t_of])
                P.add(DVE, lambda e, o3=o3: e.tensor_tensor(out=o3, in0=o3, in1=nw[:].unsqueeze(1).broadcast_to([128, 8, 128]), op=ALU.mult), reads=[t_of, t_c4], writes=[t_of])
                (dnb, t_dnb) = dnbr.next()
                P.add(DVE, lambda e, dnb=dnb, of=of, zs=zs: e.tensor_tensor(out=dnb[:], in0=of[:], in1=zs[:], op=ALU.mult), reads=[t_of, t_zs], writes=[t_dnb])
                (pb, t_pb) = ptb.next()
                for h in range(8):
                    P.add(PE, lambda e, pb=pb, dnb=dnb, h=h: e.transpose(out=pb[:, h, :], in_=dnb[:, h * 128:(h + 1) * 128], identity=C.identb[:]), reads=[t_dnb, C.t_const], writes=[t_pb])
                P.add(ACT, lambda e, pb=pb, dnT=dnT, i=i: e.activation(out=dnT[:, :, i * 128:(i + 1) * 128], in_=pb[:], func=AF.Copy), reads=[t_pb], writes=[t_dnT])
                for half in range(2):
                    (pt, t_pt) = ptr.next()
                    for j in range(4):
                        kc = half * 4 + j
                        P.add(PE, lambda e, pt=pt, xt=xt, kc=kc, j=j: e.transpose(out=pt[:, j, :], in_=xt[:, kc * 128:(kc + 1) * 128], identity=C.ident[:]), reads=[t_xt, C.t_const], writes=[t_pt])
                    P.add(ACT, lambda e, pt=pt, xT=xT, half=half, i=i: e.activation(out=xT[:, half * 4:(half + 1) * 4, i * 128:(i + 1) * 128], in_=pt[:], func=AF.Copy), reads=[t_pt], writes=[t_xT])
            for c in range(8):
                (wa, t_wa) = wload(wsm, "Wb_ao", c * 1024, 8, "Wb_ao_%d" % c)
                (wd_, t_wd_) = wload(wsm, "Wb_do", c * 1024, 8, "Wb_do_%d" % c)
                (ga, t_ga) = gar.next(); (gd, t_gd) = gdr.next()
                P.add(SP, lambda e, ga=ga, c=c, t0=t0: e.dma_start(out=ga[:], in_=S["GT"][c, :, t0:t0 + 512]), reads=bar, writes=[t_ga], dma_key="p4ga%d" % ((gar.i - 1) % 2))
                P.add(SP, lambda e, gd=gd, c=c, t0=t0: e.dma_start(out=gd[:], in_=S["GT"][8 + c, :, t0:t0 + 512]), reads=bar, writes=[t_gd], dma_key="p4gd%d" % ((gdr.i - 1) % 2))
                (pa, t_pa) = pmm.next()
                for kc in range(8):
                    P.add(PE, lambda e, pa=pa, wa=wa, kc=kc: e.matmul(out=pa[:], lhsT=wa[:, kc, :], rhs=aT[:, kc, :], start=(kc == 0), stop=(kc == 7)), reads=[t_wa, t_aT], writes=[t_pa])
                (pd, t_pd) = pmm.next()
                for kc in range(8):
                    P.add(PE, lambda e, pd=pd, wd_=wd_, kc=kc: e.matmul(out=pd[:], lhsT=wd_[:, kc, :], rhs=dnT[:, kc, :], start=(kc == 0), stop=(kc == 7)), reads=[t_wd_, t_dnT], writes=[t_pd])
                (t1, t_t1) = tr1.next(); (t2, t_t2) = tr2.next()
                P.add(DVE, lambda e, t1=t1, pa=pa, ga=ga: e.tensor_tensor(out=t1[:], in0=pa[:], in1=ga[:], op=ALU.mult), reads=[t_pa, t_ga], writes=[t_t1])
                P.add(DVE, lambda e, t2=t2, pd=pd, gd=gd: e.tensor_tensor(out=t2[:], in0=pd[:], in1=gd[:], op=ALU.mult), reads=[t_pd, t_gd], writes=[t_t2])
                P.add(DVE, lambda e, t1=t1, t2=t2, c=c: e.tensor_tensor(out=mixT[:, c, :], in0=t1[:], in1=t2[:], op=ALU.add), reads=[t_t1, t_t2], writes=[t_mixT])
            for c in range(8):
                (wo, t_wo) = wload(wsm, "Wb_out", c * 1024, 8, "Wb_out_%d" % c)
                (pm, t_pm) = pmm.next()
                for kc in range(8):
                    P.add(PE, lambda e, pm=pm, wo=wo, kc=kc: e.matmul(out=pm[:], lhsT=wo[:, kc, :], rhs=mixT[:, kc, :], start=(kc == 0), stop=(kc == 7)), reads=[t_wo, t_mixT], writes=[t_pm])
                P.add(ACT, lambda e, pm=pm, c=c: e.activation(out=moT[:, c, :], in_=pm[:], func=AF.Copy), reads=[t_pm], writes=[t_moT])
            rs, t_rs = stat_norm(moT, t_moT, 1.0 / D_MODEL)
            for c in range(8):
                (tm, t_tm) = tmr.next()
                P.add(DVE, lambda e, tm=tm, c=c, rs=rs: e.tensor_tensor(out=tm[:], in0=moT[:, c, :], in1=rs[:], op=ALU.mult), reads=[t_moT, t_rs], writes=[t_tm])
                P.add(DVE, lambda e, tm=tm, c=c: e.scalar_tensor_tensor(out=xT[:, c, :], in0=tm[:], scalar=gvec[:, c:c + 1], in1=xT[:, c, :], op0=ALU.mult, op1=ALU.add),
                      reads=[t_tm, t_c4, t_xT], writes=[t_xT])
            rs2, t_rs2 = stat_norm(xT, t_xT, 1.0 / D_MODEL)
            for c in range(8):
                P.add(DVE, lambda e, c=c, rs2=rs2: e.scalar_tensor_tensor(out=h2T[:, c, :], in0=xT[:, c, :], scalar=gvec[:, 8 + c:9 + c], in1=rs2[:], op0=ALU.mult, op1=ALU.mult),
                      reads=[t_xT, t_c4, t_rs2], writes=[t_h2T])
            for f in range(NFF):
                (wg, t_wg) = wload(wsm, "Wb_g", f * 1024, 8, "Wb_g_%d" % f)
                (wu, t_wu) = wload(wsm, "Wb_u", f * 1024, 8, "Wb_u_%d" % f)
                (pg, t_pg) = pmm.next()
                for kc in range(8):
                    P.add(PE, lambda e, pg=pg, wg=wg, kc=kc: e.matmul(out=pg[:], lhsT=wg[:, kc, :], rhs=h2T[:, kc, :], start=(kc == 0), stop=(kc == 7)), reads=[t_wg, t_h2T], writes=[t_pg])
                (pu, t_pu) = pmm.next()
                for kc in range(8):
                    P.add(PE, lambda e, pu=pu, wu=wu, kc=kc: e.matmul(out=pu[:], lhsT=wu[:, kc, :], rhs=h2T[:, kc, :], start=(kc == 0), stop=(kc == 7)), reads=[t_wu, t_h2T], writes=[t_pu])
                (sg, t_sg) = sgr.next()
                P.add(ACT, lambda e, sg=sg, pg=pg: e.activation(out=sg[:], in_=pg[:], func=AF.Silu), reads=[t_pg], writes=[t_sg])
                P.add(DVE, lambda e, sg=sg, pu=pu, f=f: e.tensor_tensor(out=actT[:, f, :], in0=pu[:], in1=sg[:], op=ALU.mult), reads=[t_pu, t_sg], writes=[t_actT])
            for c in range(8):
                (wdn, t_wdn) = wload(wdr, "Wb_d", c * NFF * 128, NFF, "Wb_d_%d" % c)
                (pf, t_pf) = pmm.next()
                for f in range(NFF):
                    P.add(PE, lambda e, pf=pf, wdn=wdn, f=f: e.matmul(out=pf[:], lhsT=wdn[:, f, :], rhs=actT[:, f, :], start=(f == 0), stop=(f == NFF - 1)), reads=[t_wdn, t_actT], writes=[t_pf])
                P.add(ACT, lambda e, pf=pf, c=c: e.activation(out=moT[:, c, :], in_=pf[:], func=AF.Copy), reads=[t_pf], writes=[t_moT])
            rs3, t_rs3 = stat_norm(moT, t_moT, 1.0 / D_MODEL)
            for c in range(8):
                (tm, t_tm) = tmr.next()
                P.add(DVE, lambda e, tm=tm, c=c, rs3=rs3: e.tensor_tensor(out=tm[:], in0=moT[:, c, :], in1=rs3[:], op=ALU.mult), reads=[t_moT, t_rs3], writes=[t_tm])
                P.add(DVE, lambda e, tm=tm, c=c: e.scalar_tensor_tensor(out=moT[:, c, :], in0=tm[:], scalar=gvec[:, 16 + c:17 + c], in1=xT[:, c, :], op0=ALU.mult, op1=ALU.add),
                      reads=[t_tm, t_c4, t_xT, t_moT], writes=[t_moT])
            for i in range(4):
                r0 = t0 + i * 128
                (yo, t_yo) = yor.next()
                for half in range(2):
                    (pt, t_pt) = ptr.next()
                    for j in range(4):
                        kc = half * 4 + j
                        P.add(PE, lambda e, pt=pt, kc=kc, j=j, i=i: e.transpose(out=pt[:, j, :], in_=moT[:, kc, i * 128:(i + 1) * 128], identity=C.ident[:]), reads=[t_moT, C.t_const], writes=[t_pt])
                    P.add(ACT, lambda e, pt=pt, yo=yo, half=half: e.activation(out=yo[:, half * 512:(half + 1) * 512], in_=pt[:].rearrange("p a c -> p (a c)"), func=AF.Copy), reads=[t_pt], writes=[t_yo])
                k = "yst%d" % (nst[0] % 4)
                nst[0] += 1
                o = P.add(POOL, lambda e, yo=yo, r0=r0: e.dma_start(out=K.y[r0:r0 + 128, :], in_=yo[:]), reads=[t_yo], dma_key=k)
        finals = [K.P.dma_last[k] for k in ("yst0", "yst1", "yst2", "yst3") if k in K.P.dma_last]
    return finals


def host_consts(T, SEG, BLK1, carry):
    c = {}
    c["ident"] = np.eye(128, dtype=np.float32)
    rm = np.zeros((128, 128), np.float32)
    for d in range(16):
        rm[d + 16, d] = -1.0
        rm[d, d + 16] = 1.0
    c["rotm"] = rm
    t = np.arange(T)
    pos = (t if carry else (t % SEG)).astype(np.float32)
    inv = (500000.0 ** (-np.arange(16, dtype=np.float32) * 2.0 / 32.0)).astype(np.float32)
    ang = pos[None, :] * inv[:, None]
    ang = np.concatenate([ang, ang], axis=0)
    c["cosT"] = np.cos(ang).astype(np.float32)
    c["sinT"] = np.sin(ang).astype(np.float32)
    NB1 = T // BLK1
    hf = np.zeros((4, NB1), np.float32)
    for b in range(NB1):
        tb0 = b * BLK1
        l = 0.0 if tb0 == 0 else (float(carry) if tb0 % SEG == 0 else 1.0)
        e = tb0 + BLK1
        r = 0.0 if e == T else (float(carry) if e % SEG == 0 else 1.0)
        hf[0:2, b] = l
        hf[2:4, b] = r
    c["hflags"] = hf
    c["carry"] = np.full((128, 1), float(carry), np.float32)
    q = np.arange(128)[:, None]
    kk = np.arange(384)[None, :]
    rel = kk - 128 - q
    base = np.where(np.abs(rel) <= 128, 0.0, NEG).astype(np.float32)
    noprev = base.copy()
    noprev[:, 0:128] = NEG
    nonext = base.copy()
    nonext[:, 256:384] = NEG
    am = np.stack([base, base if carry else noprev, base if carry else nonext, noprev, nonext], axis=1)
    c["amask"] = np.ascontiguousarray(am.astype(np.float32))
    a = np.arange(128)[:, None]
    b = np.arange(128)[None, :]
    NB_ = -1e30
    tri = np.stack([(a <= b).astype(np.float32), (a >= b).astype(np.float32),
                    np.where(a > b, 0.0, NB_), np.where(a < b, 0.0, NB_),
                    np.where(a <= b, 0.0, NB_), np.where(a >= b, 0.0, NB_)], axis=1)
    c["tri"] = np.ascontiguousarray(tri.astype(np.float32))
    return c


T_CORE, SEG_LEN, BLK1_LEN = 16384, 4096, 1024
_VEC = ("a_log", "dt_bias", "attn_sink", "dn_norm_w", "g_pre_mix", "g_post_mix", "g_pre_ffn", "g_post_ffn")
_NC_CACHE = {}


def kernel(x_prompt, x_sample, w_in, conv_w, a_log, dt_bias, attn_sink, dn_norm_w, w_attn_o, w_dn_o, w_out,
           w_gate, w_up, w_down, g_pre_mix, g_post_mix, g_pre_ffn, g_post_ffn):
    T, SEG, BLK1 = T_CORE, SEG_LEN, BLK1_LEN
    x_prompt = np.asarray(x_prompt, dtype=np.float32)
    x_sample = np.asarray(x_sample, dtype=np.float32)
    W = dict(w_in=w_in, conv_w=conv_w, a_log=a_log, dt_bias=dt_bias, attn_sink=attn_sink, dn_norm_w=dn_norm_w, w_attn_o=w_attn_o,
             w_dn_o=w_dn_o, w_out=w_out, w_gate=w_gate, w_up=w_up, w_down=w_down, g_pre_mix=g_pre_mix, g_post_mix=g_post_mix,
             g_pre_ffn=g_pre_ffn, g_post_ffn=g_post_ffn)
    Wc = {}
    for k, v in W.items():
        v = np.asarray(v, dtype=np.float32)[0]
        Wc[k] = np.ascontiguousarray(v.reshape(-1) if k in _VEC else v)
    if "nc" not in _NC_CACHE:
        _NC_CACHE["nc"] = build(T, SEG, BLK1, debug=False)
    nc = _NC_CACHE["nc"]
    consts = {1: host_consts(T, SEG, BLK1, 1), 0: host_consts(T, SEG, BLK1, 0)}
    in_maps = []
    for core in range(8):
        if core < 2:
            xs, carry = x_sample[core], 1
        else:
            pc = core - 2 if core < 6 else core - 4
            xs, carry = x_prompt[4 * pc:4 * pc + 4].reshape(T, D_MODEL), 0
        m = {"x": np.ascontiguousarray(xs)}
        m.update(Wc)
        m.update(consts[carry])
        in_maps.append(m)
    res = run_bass_kernel_spmd(nc, in_maps, core_ids=list(range(8)))
    ys = [np.asarray(res.results[c]["y"], dtype=np.float32) for c in range(8)]
    y_sample = np.stack([ys[0], ys[1]], axis=0)
    y_prompt = np.concatenate([ys[c].reshape(4, SEG, D_MODEL) for c in range(2, 6)], axis=0)
    return (y_prompt, y_sample)
```
